# Optimizing a Trainium2 kernel written in Bass

```python
import math
import jax, jax.numpy as jnp
from jax import lax
import numpy as np

D_MODEL = 2048
BATCH = 4
SEQ = 2048
DEPTH = 1
DEC_BATCH = 128
DEC_SEQ = 8
PAST_LEN = 16384
PAGE_SIZE = 128

N_MEM = 256
X_HEADS = 4
X_HEAD_DIM = D_MODEL // 16
D_XATT = X_HEADS * X_HEAD_DIM
D_CONV_A = D_MODEL
CONV_A_W = 3
EXPAND = 2
D_INNER = EXPAND * D_MODEL
SSM_HEAD_DIM = 64
SSM_HEADS = D_INNER // SSM_HEAD_DIM
SSM_GROUPS = 8
D_STATE = 128
CONV_B_W = 4
SSD_CHUNK = 128
D_XBC = D_INNER + 2 * SSM_GROUPS * D_STATE
D_FF = 4 * D_MODEL
EPS = 1e-6
SPLIT_SIZES = (D_CONV_A, D_CONV_A, D_CONV_A, D_INNER, D_XBC, SSM_HEADS, D_XATT, 3 * D_MODEL)
D_IN_PROJ = sum(SPLIT_SIZES)

kernel_name = "hybrid_conv_ssd_memxattn_step"


def _split_points(sizes):
    return [int(v) for v in np.cumsum(np.array(sizes))[:-1]]


def rmsnorm(x, g):
    xf = x.astype(jnp.float32)
    y = xf * lax.rsqrt(jnp.mean(xf * xf, axis=-1, keepdims=True) + EPS)
    return (y * g.astype(jnp.float32)).astype(x.dtype)


def gated_group_rmsnorm(y, z, g):
    b, l, d = y.shape
    u = (y.astype(jnp.float32) * jax.nn.silu(z.astype(jnp.float32))).reshape(b, l, SSM_GROUPS, d // SSM_GROUPS)
    u = u * lax.rsqrt(jnp.mean(u * u, axis=-1, keepdims=True) + EPS)
    return (u.reshape(b, l, d) * g.astype(jnp.float32)).astype(y.dtype)


def causal_dwconv(x, prev, w):
    K = w.shape[0]
    L = x.shape[1]
    u = jnp.concatenate([prev.astype(x.dtype), x], axis=1)
    y = u[:, 0:L] * w[0]
    for k in range(1, K):
        y = y + u[:, k:k + L] * w[k]
    return y, u[:, -(K - 1):]


def ssd(x, dt, A, B, C, h0):
    b, l, nh, p = x.shape
    n = B.shape[-1]
    q = math.gcd(l, SSD_CHUNK)
    c = l // q
    r = nh // SSM_GROUPS
    xr = x.reshape(b, c, q, SSM_GROUPS, r, p)
    dtr = dt.reshape(b, c, q, SSM_GROUPS, r)
    Br = B.reshape(b, c, q, SSM_GROUPS, n)
    Cr = C.reshape(b, c, q, SSM_GROUPS, n)
    acum = jnp.cumsum(dtr * A.reshape(SSM_GROUPS, r), axis=2)
    xdt = xr * dtr[..., None]
    mask = jnp.tril(jnp.ones((q, q), dtype=bool))[None, None, :, :, None, None]
    seg = acum[:, :, :, None] - acum[:, :, None, :]
    decay = jnp.exp(jnp.where(mask, seg, -jnp.inf))
    cb = jnp.einsum('bcqgn,bcsgn->bcqsg', Cr, Br)
    y_diag = jnp.einsum('bcqsgr,bcsgrp->bcqgrp', cb[..., None] * decay, xdt)
    decay_end = jnp.exp(acum[:, :, -1:] - acum)
    states = jnp.einsum('bcsgn,bcsgrp->bcgrpn', Br, xdt * decay_end[..., None])
    chunk_decay = jnp.exp(acum[:, :, -1])

    def step(h, inp):
        s, d = inp
        return h * d[..., None, None] + s, h

    h_final, h_prev = lax.scan(step, h0.reshape(b, SSM_GROUPS, r, p, n),
                               (jnp.moveaxis(states, 1, 0), jnp.moveaxis(chunk_decay, 1, 0)))
    h_prev = jnp.moveaxis(h_prev, 0, 1)
    y_off = jnp.einsum('bcqgn,bcgrpn->bcqgrp', Cr, h_prev) * jnp.exp(acum)[..., None]
    y = (y_diag + y_off).reshape(b, l, nh, p)
    return y, h_final.reshape(b, nh, p, n)


def mem_kv(mem, g, w_kv):
    b, m, _ = mem.shape
    kv = rmsnorm(mem, g) @ w_kv
    k, v = jnp.split(kv, 2, axis=-1)
    return k.reshape(b, m, X_HEADS, X_HEAD_DIM), v.reshape(b, m, X_HEADS, X_HEAD_DIM)


def cross_attn(q, k, v):
    s = jnp.einsum('blhd,bmhd->bhlm', q, k).astype(jnp.float32) * (X_HEAD_DIM ** -0.5)
    pr = jax.nn.softmax(s, axis=-1)
    return jnp.einsum('bhlm,bmhd->blhd', pr.astype(v.dtype), v)


def _layer(x, conv_a_prev, conv_b_prev, ssm_prev, mem_k, mem_v, p):
    b, l, _ = x.shape
    xn = rmsnorm(x, p['norm_mix_pre'])
    proj = xn @ p['w_in']
    b_a, c_a, h_a, z, xbc, dt_raw, q, gates = jnp.split(proj, _split_points(SPLIT_SIZES), axis=-1)
    conv_a_out, conv_a_new = causal_dwconv(c_a * h_a, conv_a_prev, p['conv_a_w'])
    y_a = (b_a * conv_a_out) @ p['w_out_a']
    xbc_c, conv_b_new = causal_dwconv(xbc, conv_b_prev, p['conv_b_w'])
    xbc_c = jax.nn.silu(xbc_c + p['conv_b_bias'])
    xs, Bs, Cs = jnp.split(xbc_c, [D_INNER, D_INNER + SSM_GROUPS * D_STATE], axis=-1)
    xs_h = xs.reshape(b, l, SSM_HEADS, SSM_HEAD_DIM).astype(jnp.float32)
    dt = jax.nn.softplus(dt_raw.astype(jnp.float32) + p['dt_bias'].astype(jnp.float32))
    A = -jnp.exp(p['a_log'].astype(jnp.float32))
    y_s, ssm_new = ssd(xs_h, dt, A,
                       Bs.reshape(b, l, SSM_GROUPS, D_STATE).astype(jnp.float32),
                       Cs.reshape(b, l, SSM_GROUPS, D_STATE).astype(jnp.float32),
                       ssm_prev.astype(jnp.float32))
    y_s = y_s + p['d_skip'].astype(jnp.float32)[:, None] * xs_h
    y_s = y_s.reshape(b, l, D_INNER).astype(x.dtype)
    y_b = gated_group_rmsnorm(y_s, z, p['ssm_norm_w']) @ p['w_out_b']
    o = cross_attn(q.reshape(b, l, X_HEADS, X_HEAD_DIM), mem_k, mem_v).reshape(b, l, D_XATT)
    y_x = o @ p['w_out_x']
    g_a, g_b, g_x = jnp.split(jax.nn.sigmoid(gates), 3, axis=-1)
    mixed = (g_a * y_a + g_b * y_b + g_x * y_x) @ p['w_o']
    x = x + rmsnorm(mixed, p['norm_mix_post'])
    hn = rmsnorm(x, p['norm_mlp_pre'])
    ff = jnp.square(jax.nn.relu(hn @ p['w_ff1'])) @ p['w_ff2']
    x = x + rmsnorm(ff, p['norm_mlp_post'])
    return x, conv_a_new, conv_b_new, ssm_new.astype(ssm_prev.dtype)


def setup_inputs(seed: int = 0) -> dict:
    key = jax.random.key(seed)
    ks = jax.random.split(key, 40)

    def nrm(k, shape, scale):
        return jax.random.normal(k, shape, jnp.float32) * scale

    def gain(k, shape):
        return 1.0 + nrm(k, shape, 0.02)

    dt0 = jnp.exp(jax.random.uniform(ks[30], (DEPTH, SSM_HEADS), jnp.float32, math.log(1e-3), math.log(1e-1)))
    dt_bias = dt0 + jnp.log(-jnp.expm1(-dt0))
    a_log = jnp.log(jax.random.uniform(ks[31], (DEPTH, SSM_HEADS), jnp.float32, 1.0, 16.0))
    return {
        "x_prompt": nrm(ks[0], (BATCH, SEQ, D_MODEL), 1.0),
        "x_sample": nrm(ks[1], (DEC_BATCH, DEC_SEQ, D_MODEL), 1.0),
        "mem_prompt": nrm(ks[2], (BATCH, N_MEM, D_MODEL), 1.0),
        "cache_mem_k": nrm(ks[3], (DEPTH, DEC_BATCH, N_MEM, X_HEADS, X_HEAD_DIM), 1.0),
        "cache_mem_v": nrm(ks[4], (DEPTH, DEC_BATCH, N_MEM, X_HEADS, X_HEAD_DIM), 1.0),
        "state_conv_a": nrm(ks[5], (DEPTH, DEC_BATCH, CONV_A_W - 1, D_CONV_A), 1.0),
        "state_conv_b": nrm(ks[6], (DEPTH, DEC_BATCH, CONV_B_W - 1, D_XBC), 1.0),
        "state_ssm": nrm(ks[7], (DEPTH, DEC_BATCH, SSM_HEADS, SSM_HEAD_DIM, D_STATE), 0.1),
        "norm_mix_pre": gain(ks[8], (DEPTH, D_MODEL)),
        "norm_mix_post": gain(ks[9], (DEPTH, D_MODEL)),
        "norm_mlp_pre": gain(ks[10], (DEPTH, D_MODEL)),
        "norm_mlp_post": gain(ks[11], (DEPTH, D_MODEL)),
        "norm_mem": gain(ks[12], (DEPTH, D_MODEL)),
        "w_in": nrm(ks[13], (DEPTH, D_MODEL, D_IN_PROJ), D_MODEL ** -0.5),
        "conv_a_w": nrm(ks[14], (DEPTH, CONV_A_W, D_CONV_A), CONV_A_W ** -0.5),
        "w_out_a": nrm(ks[15], (DEPTH, D_CONV_A, D_MODEL), D_CONV_A ** -0.5),
        "conv_b_w": nrm(ks[16], (DEPTH, CONV_B_W, D_XBC), CONV_B_W ** -0.5),
        "conv_b_bias": nrm(ks[17], (DEPTH, D_XBC), 0.02),
        "dt_bias": dt_bias,
        "a_log": a_log,
        "d_skip": 1.0 + nrm(ks[18], (DEPTH, SSM_HEADS), 0.1),
        "ssm_norm_w": gain(ks[19], (DEPTH, D_INNER)),
        "w_out_b": nrm(ks[20], (DEPTH, D_INNER, D_MODEL), D_INNER ** -0.5),
        "w_mem_kv": nrm(ks[21], (DEPTH, D_MODEL, 2 * D_XATT), D_MODEL ** -0.5),
        "w_out_x": nrm(ks[22], (DEPTH, D_XATT, D_MODEL), D_XATT ** -0.5),
        "w_o": nrm(ks[23], (DEPTH, D_MODEL, D_MODEL), D_MODEL ** -0.5),
        "w_ff1": nrm(ks[24], (DEPTH, D_MODEL, D_FF), D_MODEL ** -0.5),
        "w_ff2": nrm(ks[25], (DEPTH, D_FF, D_MODEL), D_FF ** -0.5),
    }


def reference(x_prompt, x_sample, mem_prompt, cache_mem_k, cache_mem_v, state_conv_a, state_conv_b,
              state_ssm, norm_mix_pre, norm_mix_post, norm_mlp_pre, norm_mlp_post, norm_mem, w_in,
              conv_a_w, w_out_a, conv_b_w, conv_b_bias, dt_bias, a_log, d_skip, ssm_norm_w, w_out_b,
              w_mem_kv, w_out_x, w_o, w_ff1, w_ff2):
    bp = x_prompt.shape[0]
    yp, ys = x_prompt, x_sample
    mk_p, mv_p, ca_p, cb_p, ss_p, ca_s, cb_s, ss_s = [], [], [], [], [], [], [], []
    for l in range(DEPTH):
        p = dict(norm_mix_pre=norm_mix_pre[l], norm_mix_post=norm_mix_post[l],
                 norm_mlp_pre=norm_mlp_pre[l], norm_mlp_post=norm_mlp_post[l], w_in=w_in[l],
                 conv_a_w=conv_a_w[l], w_out_a=w_out_a[l], conv_b_w=conv_b_w[l],
                 conv_b_bias=conv_b_bias[l], dt_bias=dt_bias[l], a_log=a_log[l], d_skip=d_skip[l],
                 ssm_norm_w=ssm_norm_w[l], w_out_b=w_out_b[l], w_out_x=w_out_x[l], w_o=w_o[l],
                 w_ff1=w_ff1[l], w_ff2=w_ff2[l])
        mkp, mvp = mem_kv(mem_prompt, norm_mem[l], w_mem_kv[l])
        yp, cap, cbp, ssp = _layer(
            yp,
            jnp.zeros((bp, CONV_A_W - 1, D_CONV_A), yp.dtype),
            jnp.zeros((bp, CONV_B_W - 1, D_XBC), yp.dtype),
            jnp.zeros((bp, SSM_HEADS, SSM_HEAD_DIM, D_STATE), state_ssm.dtype),
            mkp, mvp, p)
        ys, cas, cbs, sss = _layer(ys, state_conv_a[l], state_conv_b[l], state_ssm[l],
                                   cache_mem_k[l], cache_mem_v[l], p)
        mk_p.append(mkp); mv_p.append(mvp); ca_p.append(cap); cb_p.append(cbp); ss_p.append(ssp)
        ca_s.append(cas); cb_s.append(cbs); ss_s.append(sss)
    return (yp, ys, jnp.stack(mk_p), jnp.stack(mv_p), jnp.stack(ca_p), jnp.stack(cb_p), jnp.stack(ss_p),
            jnp.stack(ca_s), jnp.stack(cb_s), jnp.stack(ss_s))
```

```python
import numpy as np
from contextlib import ExitStack
import concourse.bass as bass
import concourse.mybir as mybir
from concourse.bass_utils import run_bass_kernel_spmd

F32 = mybir.dt.float32
BF16 = mybir.dt.bfloat16
ALU = mybir.AluOpType
AF = mybir.ActivationFunctionType
AX = mybir.AxisListType

NCORES = 8
D = 2048
NPREV = 1024
NMAIN = 1152
NTOK = NPREV + NMAIN
NCH_PREV = 8
NCH = 17
D_IN_PROJ = 23104
OFF_BA, OFF_CA, OFF_HA, OFF_Z, OFF_XBC, OFF_DT, OFF_Q, OFF_G = 0, 2048, 4096, 6144, 10240, 16384, 16448, 16960
EPS = 1e-6
NEG = -30000.0


class Tok:
    __slots__ = ("name", "w", "r", "dsem", "dcount", "excl")

    def __init__(self, name, excl=False):
        self.name = name
        self.excl = excl
        self.w = None
        self.r = []
        self.dsem = None
        self.dcount = 0


class Sched:
    ENGS = ("pe", "dve", "act", "pool", "sp")

    def __init__(self, nc, stack):
        self.nc = nc
        self.stack = stack
        self.ops = {e: [] for e in self.ENGS}
        self.sem = {}
        self.cnt = {e: 0 for e in self.ENGS}
        self.seen = {e: {} for e in self.ENGS}
        for e in ("pe", "dve", "act", "pool"):
            self.sem[e] = stack.enter_context(nc.semaphore("c_" + e))
        self.dma_toks = []
        self.free_dsems = []

    def _need(self, eng, ev):
        sem, val = ev
        k = id(sem)
        if self.seen[eng].get(k, 0) >= val:
            return
        self.seen[eng][k] = val
        self.ops[eng].append(("wait", sem, val))

    def _deps(self, eng, reads, writes):
        need = {}

        def add(ev):
            if ev is None:
                return
            k = id(ev[0])
            if k not in need or need[k][1] < ev[1]:
                need[k] = ev
        for t in reads:
            add(t.w)
        for t in writes:
            add(t.w)
            for ev in t.r:
                add(ev)
        for ev in need.values():
            if eng == "pe" and ev[0] is self.sem["pe"]:
                continue
            self._need(eng, ev)

    def _commit(self, ev, reads, writes):
        for t in writes:
            t.w = ev
            t.r = []
        for t in reads:
            if t not in writes:
                t.r.append(ev)
                if len(t.r) > 24:
                    best = {}
                    for e2 in t.r:
                        k = id(e2[0])
                        if k not in best or best[k][1] < e2[1]:
                            best[k] = e2
                    t.r = list(best.values())

    def op(self, eng, fn, reads=(), writes=()):
        ex = [t for t in reads if t.excl and t not in writes]
        if ex:
            reads = [t for t in reads if not t.excl]
            writes = list(writes) + ex
        self._deps(eng, reads, writes)
        self.cnt[eng] += 1
        ev = (self.sem[eng], self.cnt[eng])
        self.ops[eng].append(("op", fn, self.sem[eng]))
        self._commit(ev, reads, writes)
        return ev

    def dma(self, eng, fn, reads=(), writes=(), tok=None):
        self._deps(eng, reads, writes)
        if tok is None:
            tok = (list(writes) + list(reads))[0]
        if tok.dsem is None:
            tok.dsem = self.stack.enter_context(self.nc.semaphore("d_" + tok.name))
            self.dma_toks.append(tok)
        tok.dcount += 16
        ev = (tok.dsem, tok.dcount)
        self.ops[eng].append(("dma", fn, tok.dsem))
        self._commit(ev, reads, writes)
        return ev

    def all_events(self):
        evs = [(self.sem[e], self.cnt[e]) for e in ("pe", "dve", "act", "pool") if self.cnt[e]]
        evs += [(t.dsem, t.dcount) for t in self.dma_toks if t.dcount]
        return evs

    def barrier(self):
        evs = self.all_events()
        for e in self.ENGS:
            for ev in evs:
                if e == "pe" and ev[0] is self.sem["pe"]:
                    continue
                self._need(e, ev)

    def final_wait(self, eng="sp"):
        for ev in self.all_events():
            self._need(eng, ev)

    def emit(self):
        nc = self.nc
        handles = {"pe": "tensor", "dve": "vector", "act": "scalar", "pool": "gpsimd", "sp": "sync"}
        with nc.Block() as block:
            for e in self.ENGS:
                lst = self.ops[e]
                if not lst:
                    continue

                def body(engine, lst=lst):
                    for item in lst:
                        if item[0] == "wait":
                            engine.wait_ge(item[1], item[2])
                        elif item[0] == "op":
                            item[1](engine).then_inc(item[2], 1)
                        else:
                            item[1](engine).then_inc(item[2], 16)
                getattr(block, handles[e])(body)


class Arena:
    def __init__(self, nc, stack, nbytes):
        self.t = stack.enter_context(nc.sbuf_tensor("arena", [128, nbytes // 4], F32))
        self.top = 0
        self.hi_free = nbytes
        self.nbytes = nbytes

    def alloc(self, name, shape, dtype, top=False):
        esz = 2 if dtype == BF16 else 4
        ne = int(np.prod(shape))
        n4 = (ne * esz + 63) // 64 * 64
        assert self.top + n4 <= self.hi_free, ("SBUF arena overflow", name, self.top, n4, self.hi_free)
        if top:
            self.hi_free -= n4
            off = self.hi_free
        else:
            off = self.top
            self.top += n4
        ap = self.t[:, off // 4:(off + n4) // 4]
        if dtype != F32:
            ap = ap.bitcast(dtype)
        ap = ap[:, 0:ne]
        if len(shape) == 2:
            ap = ap.rearrange("p (a b) -> p a b", a=shape[0])
        elif len(shape) == 3:
            ap = ap.rearrange("p (a b c) -> p a b c", a=shape[0], b=shape[1])
        return ap, Tok(name)

    def mark(self):
        return self.top

    def release(self, m):
        self.top = m

    def release_top(self):
        self.hi_free = self.nbytes


PP_LAYOUT = [("ident", 128), ("triu_p", 128), ("triu_s", 128), ("ones_s", 128), ("negm_p", 128),
             ("negm_s", 128), ("seqmask", 16), ("gmlp", 16), ("gssm", 32), ("drow", 32),
             ("wca", 48), ("wcb", 192), ("bcb", 48), ("dtb", 64), ("alog", 64), ("flag", 1),
             ("gmem", 16)]
PP_OFF = {}
_o = 0
for _n, _w in PP_LAYOUT:
    PP_OFF[_n] = (_o, _w)
    _o += _w
PP_COLS = _o


def build_pp(flag, norm_mlp_pre, ssm_norm_w, d_skip, conv_a_w, conv_b_w, conv_b_bias, dt_bias, a_log, norm_mem):
    pp = np.zeros((128, PP_COLS), np.float32)

    def put(name, arr):
        o, w = PP_OFF[name]
        pp[:, o:o + w] = np.asarray(arr, np.float32).reshape(128, w)
    i = np.arange(128)
    seq = i // 8
    put("ident", np.eye(128))
    put("triu_p", (i[:, None] <= i[None, :]))
    same = seq[:, None] == seq[None, :]
    put("triu_s", (i[:, None] <= i[None, :]) & same)
    put("ones_s", same)
    put("negm_p", np.where(i[:, None] <= i[None, :], 0.0, NEG))
    put("negm_s", np.where((i[:, None] <= i[None, :]) & same, 0.0, NEG))
    put("seqmask", seq[:, None] == np.arange(16)[None, :])
    put("gmlp", norm_mlp_pre.reshape(16, 128).T)
    put("gmem", norm_mem.reshape(16, 128).T)
    put("gssm", ssm_norm_w.reshape(32, 128).T)
    put("drow", d_skip[(2 * np.arange(32)[None, :] + (i[:, None] // 64))])
    put("wca", conv_a_w.reshape(3, 16, 128).transpose(2, 1, 0).reshape(128, 48))
    put("wcb", conv_b_w.reshape(4, 48, 128).transpose(2, 1, 0).reshape(128, 192))
    put("bcb", conv_b_bias.reshape(48, 128).T)
    put("dtb", np.broadcast_to(dt_bias.reshape(1, 64), (128, 64)))
    put("alog", np.broadcast_to(a_log.reshape(1, 64), (128, 64)))
    put("flag", np.full((128, 1), flag))
    return pp


def build_nc(upto=99, dbg=None):
    nc = bass.Bass("TRN2", target_bir_lowering=False)

    def din(name, shape):
        return nc.dram_tensor(name, list(shape), F32, kind="ExternalInput").ap()

    def dout(name, shape):
        return nc.dram_tensor(name, list(shape), F32, kind="ExternalOutput").ap()

    xall = din("xall", [NTOK, D])
    pp_d = din("pp", [128, PP_COLS])
    memp = din("memp", [256, D])
    kc = din("kc", [16, 256, 512])
    vc = din("vc", [16, 256, 512])
    sca = din("sca", [32, D])
    scb = din("scb", [48, 6144])
    sssm = din("sssm", [16, 64, 64, 128])
    w_in = din("w_in", [D, D_IN_PROJ])
    w_out_a = din("w_out_a", [D, D])
    w_out_b = din("w_out_b", [4096, D])
    w_mem_kv = din("w_mem_kv", [D, 1024])
    w_out_x = din("w_out_x", [512, D])
    w_o = din("w_o", [D, D])
    w_ff1 = din("w_ff1", [D, 8192])
    w_ff2 = din("w_ff2", [8192, D])
    g_pre = din("g_pre", [1, D])
    g_post = din("g_post", [1, D])
    g_mpost = din("g_mpost", [1, D])

    y_o = dout("y", [NMAIN, D])
    memk_o = dout("memk", [256, 512])
    memv_o = dout("memv", [256, 512])
    cap_o = dout("cap", [2, D])
    cbp_o = dout("cbp", [3, 6144])
    ssp_o = dout("ssp", [64, 64, 128])
    cas_o = dout("cas", [16, 2, D])
    cbs_o = dout("cbs", [16, 3, 6144])
    sss_o = dout("sss", [16, 64, 64, 128])
    acum_scr = nc.dram_tensor("acum_scr", [9, 64, 128], F32, kind="Internal").ap()
    un_scr = nc.dram_tensor("un_scr", [32, 128, NMAIN], BF16, kind="Internal").ap()
    wcache = nc.dram_tensor("wcache", [36, 128, 8192], BF16, kind="Internal").ap()
    dbg_o = {}
    if dbg:
        for k, shp in dbg.items():
            dbg_o[k] = dout("dbg_" + k, shp)

    with ExitStack() as st:
        S = Sched(nc, st)
        A = Arena(nc, st, 207 * 1024)
        banks = []
        for i in range(8):
            t = st.enter_context(nc.psum_tensor(f"ps{i}", [128, 512], F32))
            banks.append((t, Tok(f"ps{i}", excl=True)))
        bank_i = [0]

        def bank(fixed=None):
            if fixed is None:
                b = banks[bank_i[0] % 4]
                bank_i[0] += 1
            else:
                b = banks[fixed]
            return b[0][:, :], b[1]

        def mm(out, ot, lhsT, rhs, rd, start=True, stop=True):
            S.op("pe", lambda e: e.matmul(out, lhsT, rhs, start=start, stop=stop), reads=rd, writes=[ot])

        def tr(out, ot, in_, ident, rd):
            S.op("pe", lambda e: e.transpose(out, in_, ident), reads=rd, writes=[ot])

        def dump(name, ap, tok):
            if name in dbg_o:
                S.barrier()
                S.dma("pool", lambda e: e.dma_start(out=dbg_o[name], in_=ap), reads=[tok], tok=tok)

        pp, ppt = A.alloc("pp", [PP_COLS], F32)
        S.dma("sp", lambda e: e.dma_start(out=pp, in_=pp_d), writes=[ppt])

        def P(name, a=None, b=None):
            o, w = PP_OFF[name]
            if a is None:
                return pp[:, o:o + w]
            return pp[:, o + a:o + b]
        ident = P("ident")
        identb, identbt = A.alloc("identb", [128], BF16)
        S.op("dve", lambda e: e.tensor_copy(out=identb, in_=ident), reads=[ppt], writes=[identbt])
        wcbh, wcbht = A.alloc("wcbh", [48, 4], F32)
        bcbh, bcbht = A.alloc("bcbh", [48], F32)
        S.op("dve", lambda e: e.tensor_scalar(out=wcbh, in0=P("wcb").rearrange("p (t k) -> p t k", k=4),
                                               scalar1=0.5, scalar2=None, op0=ALU.mult), reads=[ppt], writes=[wcbht])
        S.op("dve", lambda e: e.tensor_scalar(out=bcbh, in0=P("bcb"), scalar1=0.5, scalar2=None, op0=ALU.mult),
             reads=[ppt], writes=[bcbht])
        mhalf, mhalft = A.alloc("mhalf", [1], F32)
        S.op("dve", lambda e: e.memset(mhalf, -0.5), writes=[mhalft])

        NSLOT = 2
        ring = [A.alloc(f"wslab{i}", [16, 512], BF16) for i in range(NSLOT)]
        ring_i = [0]

        def wslab(src_ap, kt=16, ncols=512):
            b, bt = ring[ring_i[0] % NSLOT]
            ring_i[0] += 1
            v = b.rearrange("p a b -> p (a b)")[:, 0:kt * ncols].rearrange("p (a b) -> p a b", a=kt)
            src = src_ap.rearrange("(kt p) f -> p kt f", p=128)
            S.dma("pool", lambda e: e.dma_start(out=v, in_=src), writes=[bt])
            return v, bt

        onesp, onespt = A.alloc("onesp", [128], F32)
        S.op("dve", lambda e: e.memset(onesp, 1.0), writes=[onespt])

        def rstd_of(r, rt, ssq, ssqt, n, eps):
            S.op("act", lambda e: e.activation(out=r, in_=ssq, func=AF.Ln, scale=1.0 / n, bias=float(eps)), reads=[ssqt], writes=[rt])
            S.op("act", lambda e: e.activation(out=r, in_=r, func=AF.Exp, scale=-0.5), reads=[rt], writes=[rt])

        xnM, xnMt = A.alloc("xnM", [16, NMAIN], BF16)
        xnP2, xnP2t = A.alloc("xnP2", [16, 2], BF16)
        mP0 = A.mark()
        dt9, dt9t = A.alloc("dt9", [9, 64], F32)
        ac9, ac9t = A.alloc("ac9", [9, 64], F32)
        dA16, dA16t = A.alloc("dA16", [64], F32)
        wde_t, wde_tt = A.alloc("wde", [NCH, 64], F32)
        cdb_t, cdb_tt = A.alloc("cdb", [NCH, 64], F32)
        hT, hTt = A.alloc("hT", [8, 512], F32)
        hTts = [Tok(f"hT{g}") for g in range(8)]
        halo, halot = A.alloc("halo", [48, 3], BF16)
        mA = A.mark()
        xnP, xnPt = A.alloc("xnP", [16, NPREV], BF16, top=True)
        mB = A.mark()

        gb, gbt = A.alloc("gpre_b", [D], F32)
        S.dma("sp", lambda e: e.dma_start(out=gb, in_=g_pre.partition_broadcast(128).rearrange("p a d -> p (a d)")),
              writes=[gbt])
        n_xin = [A.alloc(f"n_xin{i}", [D], F32) for i in range(2)]
        n_xs = [A.alloc(f"n_xs{i}", [D], BF16) for i in range(2)]
        n_junk, n_junkt = A.alloc("n_junk", [D], BF16)
        n_ssq = [A.alloc(f"n_ssq{i}", [1], F32) for i in range(2)]
        n_r = [A.alloc(f"n_r{i}", [1], F32) for i in range(2)]

        def norm_to_T(src_rows, nchunks, gbc, gbct, dst_of):
            for c in range(nchunks):
                xi, xit = n_xin[c % 2]
                xb, xbt = n_xs[c % 2]
                sq, sqt = n_ssq[c % 2]
                r, rt = n_r[c % 2]
                S.dma("sp", lambda e, xi=xi, c=c: e.dma_start(out=xi, in_=src_rows[c * 128:(c + 1) * 128, :]),
                      writes=[xit])
                S.op("act", lambda e, xi=xi, sq=sq: e.activation(out=n_junk, in_=xi, func=AF.Square, accum_out=sq),
                     reads=[xit], writes=[n_junkt, sqt])
                rstd_of(r, rt, sq, sqt, D, EPS)
                S.op("dve", lambda e, xi=xi, xb=xb, r=r: e.scalar_tensor_tensor(
                    out=xb, in0=xi, scalar=r, in1=gbc, op0=ALU.mult, op1=ALU.mult),
                    reads=[xit, rt, gbct], writes=[xbt])
                for half in range(2):
                    pb, pbt = bank()
                    pbb = pb.bitcast(BF16).rearrange("p (a b) -> p a b", a=8)
                    for k in range(8):
                        kt = half * 8 + k
                        tr(pbb[:, k, :], pbt, xb[:, kt * 128:(kt + 1) * 128], identb, [xbt, identbt])
                    dst, dstt = dst_of(c, half)
                    if half == 0:
                        S.op("act", lambda e, dst=dst, pbb=pbb: e.copy(out=dst, in_=pbb), reads=[pbt], writes=[dstt])
                    else:
                        S.op("dve", lambda e, dst=dst, pbb=pbb: e.tensor_copy(out=dst, in_=pbb), reads=[pbt],
                             writes=[dstt])

        def xn_dst(c, half):
            if c < NCH_PREV:
                return xnP[:, half * 8:half * 8 + 8, c * 128:(c + 1) * 128], xnPt
            c2 = c - NCH_PREV
            return xnM[:, half * 8:half * 8 + 8, c2 * 128:(c2 + 1) * 128], xnMt
        norm_to_T(xall, NCH, gb, gbt, xn_dst)
        S.op("dve", lambda e: e.tensor_copy(out=xnP2, in_=xnP[:, :, NPREV - 2:NPREV]), reads=[xnPt], writes=[xnP2t])
        dump("xnP", xnP.rearrange("p a b -> p (a b)"), xnPt)
        dump("xnM", xnM.rearrange("p a b -> p (a b)"), xnMt)

        def xn_cols(c):
            if c < NCH_PREV:
                return xnP[:, :, c * 128:(c + 1) * 128], xnPt
            c2 = c - NCH_PREV
            return xnM[:, :, c2 * 128:(c2 + 1) * 128], xnMt

        if upto >= 2:
            wdt, wdtt = A.alloc("wdt", [16, 64], BF16)
            S.dma("pool", lambda e: e.dma_start(out=wdt, in_=w_in[:, OFF_DT:OFF_DT + 64].rearrange(
                "(kt p) f -> p kt f", p=128)), writes=[wdtt])
            dt_t, dt_tt = A.alloc("dt", [NCH, 64], F32)
            dA_t, dA_tt = A.alloc("dA", [NCH, 64], F32)
            ac_t, ac_tt = A.alloc("acum", [NCH, 64], F32)
            v_t, v_tt = A.alloc("p2v", [NCH, 64], F32)
            a_t, a_tt = A.alloc("p2a", [NCH, 64], F32)
            tot_t, tot_tt = A.alloc("p2tot", [NCH, 64], F32)
            negA, negAt = A.alloc("negA", [64], F32)
            for c0 in range(0, NCH, 8):
                n = min(8, NCH - c0)
                pb, pbt = bank()
                pv = pb.rearrange("p (a b) -> p a b", b=64)
                for c in range(c0, c0 + n):
                    xc, xct = xn_cols(c)
                    for kt in range(16):
                        mm(pv[:, c - c0, :], pbt, xc[:, kt, :], wdt[:, kt, :], [xct, wdtt], start=(kt == 0), stop=(kt == 15))
                S.op("dve", lambda e, pv=pv, c0=c0, n=n: e.tensor_tensor(
                    out=v_t[:, c0:c0 + n, :], in0=pv[:, 0:n, :],
                    in1=P("dtb").rearrange("p (a b) -> p a b", a=1).to_broadcast([128, n, 64]), op=ALU.add),
                    reads=[pbt, ppt], writes=[v_tt])
            S.op("act", lambda e: e.activation(out=a_t, in_=v_t, func=AF.Abs), reads=[v_tt], writes=[a_tt])
            S.op("act", lambda e: e.activation(out=a_t, in_=a_t, func=AF.Exp, scale=-1.0), reads=[a_tt], writes=[a_tt])
            S.op("act", lambda e: e.activation(out=a_t, in_=a_t, func=AF.Ln, bias=1.0), reads=[a_tt], writes=[a_tt])
            S.op("dve", lambda e: e.scalar_tensor_tensor(out=dt_t, in0=v_t, scalar=0.0, in1=a_t, op0=ALU.max, op1=ALU.add),
                 reads=[v_tt, a_tt], writes=[dt_tt])
            S.op("act", lambda e: e.activation(out=negA, in_=P("alog"), func=AF.Exp), reads=[ppt], writes=[negAt])
            S.op("dve", lambda e: e.tensor_scalar(out=negA, in0=negA, scalar1=-1.0, scalar2=None, op0=ALU.mult),
                 reads=[negAt], writes=[negAt])
            S.op("dve", lambda e: e.tensor_tensor(out=dA_t, in0=dt_t, in1=negA.rearrange("p (a b) -> p a b", a=1)
                                                   .to_broadcast([128, NCH, 64]), op=ALU.mult),
                 reads=[dt_tt, negAt], writes=[dA_tt])
            for (dst, dstt, lp, ls) in ((ac_t, ac_tt, P("triu_p"), P("triu_s")), (tot_t, tot_tt, onesp, P("ones_s"))):
                for c0 in range(0, NCH, 8):
                    n = min(8, NCH - c0)
                    pb, pbt = bank()
                    lhs = ls if c0 == 16 else lp
                    mm(pb[:, 0:n * 64], pbt, lhs, dA_t[:, c0:c0 + n, :].rearrange("p a b -> p (a b)"),
                       [ppt, onespt, dA_tt])
                    S.op("act", lambda e, pb=pb, dst=dst, c0=c0, n=n: e.copy(
                        out=dst[:, c0:c0 + n, :].rearrange("p a b -> p (a b)"), in_=pb[:, 0:n * 64]),
                        reads=[pbt], writes=[dstt])
            S.op("dve", lambda e: e.tensor_tensor(out=wde_t, in0=tot_t, in1=ac_t, op=ALU.subtract),
                 reads=[tot_tt, ac_tt], writes=[wde_tt])
            S.op("act", lambda e: e.activation(out=wde_t, in_=wde_t, func=AF.Exp), reads=[wde_tt], writes=[wde_tt])
            S.op("dve", lambda e: e.tensor_tensor(out=wde_t, in0=wde_t, in1=dt_t, op=ALU.mult),
                 reads=[wde_tt, dt_tt], writes=[wde_tt])
            S.op("act", lambda e: e.activation(out=cdb_t, in_=tot_t, func=AF.Exp), reads=[tot_tt], writes=[cdb_tt])
            acT, acTt = A.alloc("acT", [9, 128], F32)
            scrt = Tok("acum_scr")
            for c0 in range(0, 9, 4):
                n = min(4, 9 - c0)
                pb, pbt = bank()
                for c in range(c0, c0 + n):
                    tr(pb[0:64, (c - c0) * 128:(c - c0 + 1) * 128], pbt, ac_t[:, NCH_PREV + c, :], ident, [ac_tt, ppt])
                S.op("dve", lambda e, pb=pb, c0=c0, n=n: e.tensor_copy(
                    out=acT[0:64, c0:c0 + n, :].rearrange("p a b -> p (a b)"), in_=pb[0:64, 0:n * 128]),
                    reads=[pbt], writes=[acTt])
            S.dma("sp", lambda e: e.dma_start(out=acum_scr.rearrange("c h q -> h c q"), in_=acT[0:64, :, :]),
                  reads=[acTt], writes=[scrt], tok=scrt)
            S.op("dve", lambda e: e.tensor_copy(out=dt9, in_=dt_t[:, NCH_PREV:NCH, :]), reads=[dt_tt], writes=[dt9t])
            S.op("dve", lambda e: e.tensor_copy(out=ac9, in_=ac_t[:, NCH_PREV:NCH, :]), reads=[ac_tt], writes=[ac9t])
            S.op("dve", lambda e: e.tensor_copy(out=dA16, in_=dA_t[:, 16, :]), reads=[dA_tt], writes=[dA16t])
            dump("dt", dt_t.rearrange("p a b -> p (a b)"), dt_tt)
            dump("acum", ac_t.rearrange("p a b -> p (a b)"), ac_tt)
            dump("wde", wde_t.rearrange("p a b -> p (a b)"), wde_tt)
            dump("cdb", cdb_t.rearrange("p a b -> p (a b)"), cdb_tt)
        S.barrier()
        A.release(mB)

        if upto >= 3:
            RAWW = 3 + 1024 + 176
            raws = [A.alloc(f"raw{i}", [6, RAWW], BF16) for i in range(2)]
            cvbufs = [A.alloc(f"cv{i}", [NMAIN], F32) for i in range(2)]
            th, tht = A.alloc("th", [NMAIN], F32)
            gcount = [0]
            xdtd = [A.alloc(f"xdtd{i}", [512], BF16) for i in range(2)]
            btm = [A.alloc(f"btm{i}", [128], BF16) for i in range(2)]
            for i in range(2):
                S.op("dve", lambda e, i=i: e.memset(raws[i][0], 0.0), writes=[raws[i][1]])
            S.op("dve", lambda e: e.memset(hT, 0.0), writes=hTts)
            s2 = {}

            def alloc_s2():
                s2["xdtz"] = [A.alloc(f"xdtz{i}", [8, 128], BF16) for i in range(2)]
                s2["cbT"] = [A.alloc(f"cbT{i}", [128], F32) for i in range(2)]
                s2["acb"] = [A.alloc(f"acb{i}", [8, 128], F32) for i in range(2)]
                s2["ee"] = [A.alloc(f"ee{i}", [8, 128], BF16) for i in range(2)]
                s2["MT"] = [A.alloc(f"MT{i}", [8, 128], BF16) for i in range(2)]
                s2["ea2"] = [A.alloc(f"ea2{i}", [4, 128], F32) for i in range(2)]
                s2["t1"] = [A.alloc(f"t1{i}", [4, 128], F32) for i in range(2)]
                s2["hbf"] = [A.alloc(f"hbf{i}", [512], BF16) for i in range(2)]
                s2["ych"] = [A.alloc(f"ych{i}", [4, 128], BF16) for i in range(2)]
                s2["h0n"] = [A.alloc(f"h0n{i}", [4, 128], F32) for i in range(4)]
                s2["h0T"] = [A.alloc(f"h0T{i}", [512], BF16) for i in range(2)]
                s2["Bm"] = A.alloc("Bm", [16, 128], BF16)
                s2["dAx"] = A.alloc("dAx", [8, 64], F32)
                s2["cdn"] = A.alloc("cdn", [4, 16], F32)
                s2["tailf"] = A.alloc("tailf", [6, 51], F32)
                s2["tailT"] = A.alloc("tailT", [6, 128], F32)
                s2["sc_in"] = (th[:, 0:768].rearrange("p (a b) -> p a b", a=6), tht)
                s2["hnat"] = A.alloc("hnat", [4, 128], F32)
                s2["onesb"] = A.alloc("onesb", [128], BF16)
                S.op("dve", lambda e: e.memset(s2["onesb"][0], 1.0), writes=[s2["onesb"][1]])
                for i in range(2):
                    S.op("dve", lambda e, i=i: e.memset(s2["xdtz"][i][0], 0.0), writes=[s2["xdtz"][i][1]])
            unscrt = Tok("un_scr")
            ssm_in = sssm.rearrange("b (G j hh) p n -> b (hh p) G j n", G=8, j=4, hh=2)
            sss_v = sss_o.rearrange("b (G j hh) p n -> b (hh p) G j n", G=8, j=4, hh=2)
            ssp_v = ssp_o.rearrange("(G j hh) p n -> (hh p) G j n", G=8, j=4, hh=2)
            cbs_v = cbs_o.rearrange("b k f -> (b k) f")
            cnt = [0]

            def tile_global(g, i):
                return g * 4 + i if i < 4 else (32 + g if i == 4 else 40 + g)

            def ssd_group(g, mode, pump):
                main = mode == "main"
                raw, rawt = raws[gcount[0] % 2]
                gcount[0] += 1

                def X(i, c):
                    return raw[:, i, 3 + c * 128:3 + (c + 1) * 128]
                if main:
                    xdtz, cbT, acb, ee, MT, t1, hbf, ych = (s2[k] for k in ("xdtz", "cbT", "acb", "ee", "MT", "t1", "hbf", "ych"))
                    h0n, h0T, ea2 = s2["h0n"], s2["h0T"], s2["ea2"]
                    (Bm, Bmt) = s2["Bm"]
                    (dAx, dAxt), (cdn, cdnt), (tailf, tailft), (tailT, tailTt) = s2["dAx"], s2["cdn"], s2["tailf"], s2["tailT"]
                    (sc_in, sc_int), (hnat, hnatt), (onesb, onesbt) = s2["sc_in"], s2["hnat"], s2["onesb"]
                wx, wxt = wslab(w_in[:, OFF_XBC + g * 512:OFF_XBC + (g + 1) * 512])
                b_, bt_ = ring[ring_i[0] % NSLOT]
                ring_i[0] += 1
                wbc = b_.rearrange("p a b -> p (a b)")[:, 0:16 * 256].rearrange("p (a b) -> p a b", a=16)
                for q, off in ((0, OFF_XBC + 4096 + g * 128), (1, OFF_XBC + 5120 + g * 128)):
                    S.dma("pool", lambda e, q=q, off=off: e.dma_start(
                        out=wbc[:, :, q * 128:(q + 1) * 128],
                        in_=w_in[:, off:off + 128].rearrange("(kt p) f -> p kt f", p=128)), writes=[bt_])
                wbct = bt_
                src, srct = (xnM, xnMt) if main else (xnP, xnPt)
                blocks = [(0, 512), (512, 512)] + ([(1024, 128)] if main else [])
                if main:
                    S.dma("sp", lambda e: e.dma_start(out=sc_in[0:48, 0:4, :],
                                                      in_=scb[:, g * 512:(g + 1) * 512].rearrange("r (t f) -> r t f", t=4)),
                          writes=[sc_int])
                    S.dma("sp", lambda e: e.dma_start(out=sc_in[0:48, 4, :], in_=scb[:, 4096 + g * 128:4096 + (g + 1) * 128]),
                          writes=[sc_int])
                    S.dma("sp", lambda e: e.dma_start(out=sc_in[0:48, 5, :], in_=scb[:, 5120 + g * 128:5120 + (g + 1) * 128]),
                          writes=[sc_int])
                    pb, pbt = bank()
                    for i in range(6):
                        tr(pb[:, i * 48:(i + 1) * 48], pbt, sc_in[0:48, i, :], ident[0:48, 0:48], [sc_int, ppt])
                    for i in range(6):
                        rs = raw[:, i, 1027:1203].rearrange("p (b k) -> p b k", k=11)
                        S.op("act", lambda e, pb=pb, rs=rs, i=i: e.copy(
                            out=rs[:, :, 0:3], in_=pb[:, i * 48:(i + 1) * 48].rearrange("p (b k) -> p b k", k=3)),
                            reads=[pbt], writes=[rawt])
                    yield
                def conv_tile(i):
                    tg = tile_global(g, i)
                    if not main:
                        S.op("dve", lambda e, i=i, tg=tg: e.tensor_copy(out=halo[:, tg, :], in_=raw[:, i, 1024:1027]),
                             reads=[rawt], writes=[halot])
                        if i == 5:
                            return False
                    else:
                        S.op("dve", lambda e, i=i, tg=tg: e.tensor_copy(out=raw[:, i, 0:3], in_=halo[:, tg, :]),
                             reads=[halot], writes=[rawt])
                    cv, cvt = cvbufs[i % 2]
                    S.op("act", lambda e, i=i, tg=tg, cv=cv: e.activation(out=cv[:, 0:1024], in_=raw[:, i, 3:1027], func=AF.Identity,
                                                                           scale=wcbh[:, tg, 3:4], bias=bcbh[:, tg:tg + 1]),
                         reads=[rawt, wcbht, bcbht], writes=[cvt])
                    for k in range(3):
                        S.op("dve", lambda e, i=i, k=k, tg=tg, cv=cv: e.scalar_tensor_tensor(
                            out=cv[:, 0:1024], in0=raw[:, i, k:k + 1024], scalar=wcbh[:, tg, k:k + 1], in1=cv[:, 0:1024],
                            op0=ALU.mult, op1=ALU.add), reads=[rawt, wcbht, cvt], writes=[cvt])
                    W = 1024
                    if main:
                        W = NMAIN
                        rs = raw[:, i, 1027:1203].rearrange("p (b k) -> p b k", k=11)
                        cvs = cv[:, 1024:1152].rearrange("p (b k) -> p b k", k=8)
                        S.op("act", lambda e, rs=rs, cvs=cvs, tg=tg: e.activation(
                            out=cvs, in_=rs[:, :, 3:11], func=AF.Identity, scale=wcbh[:, tg, 3:4], bias=bcbh[:, tg:tg + 1]),
                            reads=[rawt, wcbht, bcbht], writes=[cvt])
                        for k in range(3):
                            S.op("dve", lambda e, rs=rs, cvs=cvs, k=k, tg=tg: e.scalar_tensor_tensor(
                                out=cvs, in0=rs[:, :, k:k + 8], scalar=wcbh[:, tg, k:k + 1], in1=cvs,
                                op0=ALU.mult, op1=ALU.add), reads=[rawt, wcbht, cvt], writes=[cvt])
                    S.op("act", lambda e, W=W, cv=cv: e.activation(out=th[:, 0:W], in_=cv[:, 0:W], func=AF.Tanh),
                         reads=[cvt], writes=[tht])
                    S.op("dve", lambda e, i=i, W=W, cv=cv: e.scalar_tensor_tensor(
                        out=raw[:, i, 3:3 + W], in0=th[:, 0:W], scalar=1.0, in1=cv[:, 0:W], op0=ALU.add, op1=ALU.mult),
                        reads=[tht, cvt], writes=[rawt])
                    return True

                for i in range(6):
                    tg = tile_global(g, i)
                    for (t0, tn) in blocks:
                        pb, pbt = bank()
                        for kt in range(16):
                            lhs = wx[:, kt, i * 128:(i + 1) * 128] if i < 4 else wbc[:, kt, (i - 4) * 128:(i - 3) * 128]
                            mm(pb[:, 0:tn], pbt, lhs, src[:, kt, t0:t0 + tn], [wxt if i < 4 else wbct, srct],
                               start=(kt == 0), stop=(kt == 15))
                        if t0 < 1024:
                            S.op("act", lambda e, pb=pb, i=i, t0=t0: e.copy(out=raw[:, i, 3 + t0:3 + t0 + 512], in_=pb),
                                 reads=[pbt], writes=[rawt])
                            if main and t0 == 512:
                                S.op("dve", lambda e, pb=pb, i=i: e.tensor_copy(out=tailf[:, i, 0:3], in_=pb[:, 509:512]),
                                     reads=[pbt], writes=[tailft])
                        else:
                            pv = pb[:, 0:128].rearrange("p (b k) -> p b k", k=8)
                            rs = raw[:, i, 1027:1203].rearrange("p (b k) -> p b k", k=11)
                            S.op("act", lambda e, pv=pv, rs=rs: e.copy(out=rs[:, :, 3:11], in_=pv), reads=[pbt], writes=[rawt])
                            S.op("dve", lambda e, pv=pv, i=i: e.tensor_copy(
                                out=tailf[:, i, 3:51].rearrange("p (b k) -> p b k", k=3), in_=pv[:, :, 5:8]),
                                reads=[pbt], writes=[tailft])
                        yield
                    if i >= 1 and conv_tile(i - 1):
                        yield
                if conv_tile(5):
                    yield
                if main:
                    for (i0, n_) in ((0, 4), (4, 2)):
                        pb, pbt = bank()
                        for i in range(i0, i0 + n_):
                            tr(pb[0:51, (i - i0) * 128:(i - i0 + 1) * 128], pbt, tailf[:, i, :], ident, [tailft, ppt])
                        S.op("dve", lambda e, pb=pb, i0=i0, n_=n_: e.tensor_copy(
                            out=tailT[0:51, i0:i0 + n_, :].rearrange("p a b -> p (a b)"), in_=pb[0:51, 0:n_ * 128]),
                            reads=[pbt], writes=[tailTt])
                    for (i0, n_, c0) in ((0, 4, g * 512), (4, 1, 4096 + g * 128), (5, 1, 5120 + g * 128)):
                        S.dma("sp", lambda e, i0=i0, n_=n_, c0=c0: e.dma_start(
                            out=cbp_o[:, c0:c0 + n_ * 128], in_=tailT[0:3, i0:i0 + n_, :].rearrange("p a b -> p (a b)")),
                            reads=[tailTt], tok=tailTt)
                        S.dma("sp", lambda e, i0=i0, n_=n_, c0=c0: e.dma_start(
                            out=cbs_v[:, c0:c0 + n_ * 128], in_=tailT[3:51, i0:i0 + n_, :].rearrange("p a b -> p (a b)")),
                            reads=[tailTt], tok=tailTt)
                    yield
                yield "A_done"
                hTg = hT[:, g, :]
                hTgt = hTts[g]
                hsl = slice(g * 8, (g + 1) * 8)
                if main:
                    hb, hbt = hbf[0]
                    S.op("act", lambda e, hb=hb: e.copy(out=hb, in_=hTg), reads=[hTgt], writes=[hbt])
                nchunk = 9 if main else 8
                st = {}

                def load_h0(b):
                    hn, hnt = h0n[b % 4]
                    S.dma("sp", lambda e, hn=hn, b=b: e.dma_start(out=hn, in_=ssm_in[b, :, g, :, :]), writes=[hnt])

                def front(c):
                    gc = c + (NCH_PREV if main else 0)
                    k2 = cnt[0] % 2
                    cnt[0] += 1
                    sample = main and c == 8
                    d = st[c] = {"k2": k2}
                    pb, pbt = bank()
                    pbb = pb.bitcast(BF16).rearrange("p (a b) -> p a b", a=8)
                    for i in range(5):
                        tr(pbb[:, i, :], pbt, X(i, c), identb, [rawt, identbt])
                    xd, xdt_ = xdtd[k2]
                    bm_, bmt_ = btm[k2]
                    xs_tm = pbb[:, 0:4, :].rearrange("p a (h q) -> p (a h) q", h=2)
                    if main:
                        xz, xzt = xdtz[k2]
                        for hh in range(2):
                            S.op("dve", lambda e, xz=xz, hh=hh, xs_tm=xs_tm, c=c: e.tensor_tensor(
                                out=xz[:, hh::2, hh * 64:(hh + 1) * 64], in0=xs_tm[:, hh::2, :],
                                in1=dt9[:, c, g * 8 + hh:(g + 1) * 8:2].rearrange("p (h o) -> p h o", o=1).to_broadcast([128, 4, 64]),
                                op=ALU.mult), reads=[pbt, dt9t], writes=[xzt])
                    S.op("dve", lambda e, xd=xd, xs_tm=xs_tm, gc=gc: e.tensor_tensor(
                        out=xd.rearrange("p (h q) -> p h q", h=8), in0=xs_tm,
                        in1=wde_t[:, gc, hsl].rearrange("p (h o) -> p h o", o=1).to_broadcast([128, 8, 64]), op=ALU.mult),
                        reads=[pbt, wde_tt], writes=[xdt_])
                    S.op("act", lambda e, bm_=bm_, pbb=pbb: e.copy(out=bm_, in_=pbb[:, 4, :]), reads=[pbt], writes=[bmt_])
                    if not main:
                        return
                    pc, pct = bank()
                    mm(pc[:, 0:128], pct, X(4, c), X(5, c), [rawt])
                    cb_, cbt_ = cbT[k2]
                    S.op("act", lambda e, cb_=cb_, pc=pc: e.copy(out=cb_, in_=pc[:, 0:128]), reads=[pct], writes=[cbt_])
                    ab, abt = acb[k2]
                    if c < 2:
                        S.dma("sp", lambda e, ab=ab, c=c: e.dma_start(out=ab, in_=acum_scr[c, hsl, :].partition_broadcast(128)),
                              reads=[scrt], writes=[abt])
                    ea, eat = ea2[k2]
                    S.op("act", lambda e, ab=ab, ea=ea: e.activation(out=ea[0:64], in_=ab[0:64, 0::2, :], func=AF.Exp),
                         reads=[abt], writes=[eat])
                    S.op("act", lambda e, ab=ab, ea=ea: e.activation(out=ea[64:128], in_=ab[64:128, 1::2, :], func=AF.Exp),
                         reads=[abt], writes=[eat])
                    negm = P("negm_s") if sample else P("negm_p")
                    S.op("pool", lambda e, ab=ab, negm=negm: e.tensor_tensor(
                        out=ab, in0=ab, in1=negm.rearrange("p (o q) -> p o q", o=1).to_broadcast([128, 8, 128]), op=ALU.add),
                        reads=[abt, ppt], writes=[abt])
                    yield
                    S.op("dve", lambda e, ab=ab, c=c: e.tensor_tensor(
                        out=ab, in0=ab, in1=ac9[:, c, hsl].rearrange("p (h o) -> p h o", o=1).to_broadcast([128, 8, 128]),
                        op=ALU.subtract), reads=[abt, ac9t], writes=[abt])
                    e_, et_ = ee[k2]
                    S.op("act", lambda e, ab=ab, e_=e_: e.activation(out=e_, in_=ab, func=AF.Exp), reads=[abt], writes=[et_])
                    if c + 2 < nchunk:
                        S.dma("sp", lambda e, ab=ab, c=c: e.dma_start(out=ab, in_=acum_scr[c + 2, hsl, :].partition_broadcast(128)),
                              reads=[scrt], writes=[abt])
                    m_, mt_ = MT[k2]
                    S.op("pool", lambda e, m_=m_, e_=e_, cb_=cb_: e.tensor_tensor(
                        out=m_, in0=e_, in1=cb_.rearrange("p (o q) -> p o q", o=1).to_broadcast([128, 8, 128]), op=ALU.mult),
                        reads=[et_, cbt_], writes=[mt_])
                    if sample:
                        for b in range(4):
                            load_h0(b)

                def back(c):
                    gc = c + (NCH_PREV if main else 0)
                    k2 = st[c]["k2"]
                    sample = main and c == 8
                    xd, xdt_ = xdtd[k2]
                    bm_, bmt_ = btm[k2]
                    if main:
                        xz, xzt = xdtz[k2]
                        m_, mt_ = MT[k2]
                        ea, eat = ea2[k2]
                        pd, pdt = bank(4 + k2)
                        for j in range(4):
                            for hh in range(2):
                                mm(pd[:, j * 128:(j + 1) * 128], pdt, xz[:, 2 * j + hh, :], m_[:, 2 * j + hh, :], [xzt, mt_],
                                   start=(hh == 0), stop=(hh == 1))
                        po, pot = bank(6 + k2)
                        if not sample:
                            hb, hbt = hbf[c % 2]
                            for j in range(4):
                                mm(po[:, j * 128:(j + 1) * 128], pot, hb[:, j * 128:(j + 1) * 128], X(5, c), [hbt, rawt])
                        else:
                            S.op("dve", lambda e, bm_=bm_: e.tensor_tensor(
                                out=Bm, in0=bm_.rearrange("p (o n) -> p o n", o=1).to_broadcast([128, 16, 128]),
                                in1=P("seqmask").rearrange("p (b o) -> p b o", o=1).to_broadcast([128, 16, 128]), op=ALU.mult),
                                reads=[bmt_, ppt], writes=[Bmt])
                            S.op("dve", lambda e: e.tensor_copy(
                                out=dAx, in_=dA16[:, hsl].rearrange("p (h o) -> p h o", o=1).to_broadcast([128, 8, 64])),
                                reads=[dA16t], writes=[dAxt])
                            pq, pqt = bank()
                            dAxf = dAx.rearrange("p h q -> p (h q)")
                            for j in range(4):
                                mm(pq[:, j * 16:(j + 1) * 16], pqt, dAxf[:, j * 128:(j + 1) * 128], P("seqmask"), [dAxt, ppt])
                            S.op("act", lambda e, pq=pq: e.activation(out=cdn.rearrange("p a b -> p (a b)"), in_=pq[:, 0:64],
                                                                      func=AF.Exp), reads=[pqt], writes=[cdnt])
                            for b in range(16):
                                hn, hnt = h0n[b % 4]
                                h0, h0t = h0T[b % 2]
                                ptr, ptrt = bank()
                                for j in range(4):
                                    tr(ptr[:, j * 128:(j + 1) * 128], ptrt, hn[:, j, :], ident, [hnt, ppt])
                                S.op("act", lambda e, h0=h0, ptr=ptr: e.copy(out=h0, in_=ptr), reads=[ptrt], writes=[h0t])
                                for j in range(4):
                                    mm(po[:, j * 128 + 8 * b:j * 128 + 8 * b + 8], pot, h0[:, j * 128:(j + 1) * 128],
                                       raw[:, 5, 3 + c * 128 + 8 * b:3 + c * 128 + 8 * b + 8], [h0t, rawt])
                                pn, pnt = bank()
                                for j in range(4):
                                    mm(pn[:, j * 128:(j + 1) * 128], pnt, xd[:, j * 128:(j + 1) * 128], Bm[:, b, :], [xdt_, Bmt])
                                S.op("dve", lambda e, hn=hn, b=b: e.tensor_tensor(
                                    out=hn, in0=hn, in1=cdn[:, :, b:b + 1].to_broadcast([128, 4, 128]), op=ALU.mult),
                                    reads=[hnt, cdnt], writes=[hnt])
                                S.op("dve", lambda e, hn=hn, pn=pn: e.tensor_tensor(
                                    out=hn, in0=hn, in1=pn.rearrange("p (a b) -> p a b", a=4), op=ALU.add),
                                    reads=[hnt, pnt], writes=[hnt])
                                S.dma("sp", lambda e, hn=hn, b=b: e.dma_start(out=sss_v[b, :, g, :, :], in_=hn), reads=[hnt],
                                      tok=hnt)
                                if b + 4 < 16:
                                    load_h0(b + 4)
                        t_, tt_ = t1[k2]
                        S.op("dve", lambda e, t_=t_, po=po, ea=ea: e.tensor_tensor(
                            out=t_, in0=po.rearrange("p (a b) -> p a b", a=4), in1=ea, op=ALU.mult),
                            reads=[pot, eat], writes=[tt_])
                        S.op("dve", lambda e, t_=t_, pd=pd: e.tensor_tensor(
                            out=t_, in0=t_, in1=pd.rearrange("p (a b) -> p a b", a=4), op=ALU.add),
                            reads=[tt_, pdt], writes=[tt_])
                        yc, yct = ych[k2]
                        for j in range(4):
                            S.op("dve", lambda e, t_=t_, j=j, c=c, yc=yc: e.scalar_tensor_tensor(
                                out=yc[:, j, :], in0=X(j, c), scalar=P("drow", g * 4 + j, g * 4 + j + 1),
                                in1=t_[:, j, :], op0=ALU.mult, op1=ALU.add), reads=[rawt, ppt, tt_], writes=[yct])
                        S.dma("sp", lambda e, yc=yc, c=c: e.dma_start(
                            out=un_scr[g * 4:(g + 1) * 4, :, c * 128:(c + 1) * 128].rearrange("t p n -> p t n"), in_=yc),
                            reads=[yct], writes=[unscrt], tok=yct)
                    if not sample:
                        pst, pstt = bank()
                        mm(pst, pstt, bm_, xd, [bmt_, xdt_])
                        S.op("pool", lambda e, gc=gc: e.tensor_tensor(
                            out=hTg.rearrange("p (h q) -> p h q", h=8), in0=hTg.rearrange("p (h q) -> p h q", h=8),
                            in1=cdb_t[:, gc, hsl].rearrange("p (h o) -> p h o", o=1).to_broadcast([128, 8, 64]), op=ALU.mult),
                            reads=[hTgt, cdb_tt], writes=[hTgt])
                        S.op("dve", lambda e, pst=pst: e.tensor_tensor(out=hTg, in0=hTg, in1=pst, op=ALU.add),
                             reads=[hTgt, pstt], writes=[hTgt])
                        if main and c < 7:
                            hb, hbt = hbf[(c + 1) % 2]
                            S.op("act", lambda e, hb=hb: e.copy(out=hb, in_=hTg), reads=[hTgt], writes=[hbt])
                        if main and c == 7:
                            pf, pft = bank()
                            for j in range(4):
                                tr(pf[:, j * 128:(j + 1) * 128], pft, hTg[:, j * 128:(j + 1) * 128], ident, [hTgt, ppt])
                            S.op("act", lambda e, pf=pf: e.copy(out=hnat.rearrange("p a b -> p (a b)"), in_=pf),
                                 reads=[pft], writes=[hnatt])
                            S.dma("sp", lambda e: e.dma_start(out=ssp_v[:, g, :, :], in_=hnat), reads=[hnatt], tok=hnatt)

                def run_rest(it):
                    for _ in it:
                        pass
                f0 = front(0)
                run_rest(f0)
                for c in range(nchunk):
                    fn = front(c + 1) if c + 1 < nchunk else iter(())
                    next(fn, None)
                    pump()
                    run_rest(fn)
                    back(c)
                if not main:
                    S.op("dve", lambda e: e.tensor_scalar(out=hTg, in0=hTg, scalar1=P("flag"), scalar2=None, op0=ALU.mult),
                         reads=[hTgt, ppt], writes=[hTgt])

            def drive(mode):
                its = {}

                def make_pump(g):
                    def pump():
                        nx = its.get(g + 1)
                        if nx is None or nx["done"]:
                            return
                        for _ in range(3):
                            if next(nx["it"]) == "A_done":
                                nx["done"] = True
                                return
                    return pump
                for g in range(8):
                    its[g] = {"it": ssd_group(g, mode, make_pump(g)), "done": False}
                for g in range(8):
                    cur = its[g]
                    while not cur["done"]:
                        if next(cur["it"]) == "A_done":
                            cur["done"] = True
                    for _ in cur["it"]:
                        pass

            drive("prev")
            dump("hT", hT.rearrange("p a b -> p (a b)"), hTts[7])
            if upto >= 4:
                S.barrier()
                A.release_top()
                alloc_s2()
                drive("main")
        S.barrier()
        A.release(mP0)

        BLK = ((0, 512), (512, 512), (1024, 128))

        def half_slab(src_cols_ap, kt=16):
            b_, bt_ = ring[ring_i[0] % NSLOT]
            ring_i[0] += 1
            v = b_.rearrange("p a b -> p (a b)")[:, 0:kt * 256].rearrange("p (a b) -> p a b", a=kt)
            S.dma("pool", lambda e: e.dma_start(out=v, in_=src_cols_ap.rearrange("(kt p) f -> p kt f", p=128)), writes=[bt_])
            return v, bt_

        def acc_banks(fi):
            return [(banks[2 * fi][0][:, 0:512], banks[2 * fi][1]), (banks[2 * fi + 1][0][:, 0:512], banks[2 * fi + 1][1]),
                    (banks[4][0][:, fi * 128:(fi + 1) * 128], banks[4][1])]

        def pair_proj(w, wt, nkt, rhs_of, rhs_toks, kt0=0, first=True, last=True):
            for fi in range(2):
                acc = acc_banks(fi)
                for bi, (t0, tn) in enumerate(BLK):
                    for kt in range(nkt):
                        mm(acc[bi][0], acc[bi][1], w[:, kt, fi * 128:(fi + 1) * 128], rhs_of(kt0 + kt, t0, tn), [wt] + rhs_toks,
                           start=(first and kt == 0 and not (bi == 2 and fi == 1)), stop=(last and kt == nkt - 1))

        if upto >= 5:

            def gate_pair(fp, goff):
                wg, wgt = half_slab(w_in[:, goff + fp * 256:goff + (fp + 1) * 256])
                pair_proj(wg, wgt, 16, lambda kt, t0, tn: xnM[:, kt, t0:t0 + tn], [xnMt])
                for fi in range(2):
                    acc = acc_banks(fi)
                    for bi, (t0, tn) in enumerate(BLK):
                        S.op("act", lambda e, a=acc[bi][0], fi=fi, t0=t0, tn=tn: e.activation(
                            out=tg_[:, fi, t0:t0 + tn], in_=a, func=AF.Tanh, scale=0.5), reads=[acc[bi][1]], writes=[tgt_])

            def merge_pair(fp, first_term):
                for fi in range(2):
                    acc = acc_banks(fi)
                    f = fp * 2 + fi
                    for bi, (t0, tn) in enumerate(BLK):
                        if first_term:
                            S.op("dve", lambda e, a=acc[bi][0], fi=fi, f=f, t0=t0, tn=tn: e.scalar_tensor_tensor(
                                out=mixedT[:, f, t0:t0 + tn], in0=tg_[:, fi, t0:t0 + tn], scalar=1.0, in1=a,
                                op0=ALU.add, op1=ALU.mult), reads=[tgt_, acc[bi][1]], writes=[mixedTt])
                        else:
                            S.op("dve", lambda e, a=acc[bi][0], fi=fi, t0=t0, tn=tn: e.scalar_tensor_tensor(
                                out=mtmp[:, t0:t0 + tn], in0=tg_[:, fi, t0:t0 + tn], scalar=1.0, in1=a,
                                op0=ALU.add, op1=ALU.mult), reads=[tgt_, acc[bi][1]], writes=[mtmpt])
                            S.op("dve", lambda e, f=f, t0=t0, tn=tn: e.tensor_tensor(
                                out=mixedT[:, f, t0:t0 + tn], in0=mixedT[:, f, t0:t0 + tn], in1=mtmp[:, t0:t0 + tn], op=ALU.add),
                                reads=[mtmpt, mixedTt], writes=[mixedTt])

            unT, unTt = A.alloc("unT", [32, NMAIN], BF16, top=True)
            mG = A.mark()
            ug, ugt = A.alloc("ug", [4, NMAIN], F32)
            u2, u2t = A.alloc("u2", [4, NMAIN], BF16)
            gth, gtht = A.alloc("gth", [512], F32)
            gzz, gzzt = A.alloc("gzz", [512], F32)
            grs, grst = A.alloc("grs", [NMAIN], F32)
            onesb, onesbt = A.alloc("onesb5", [128], BF16)
            S.op("dve", lambda e: e.memset(onesb, 1.0), writes=[onesbt])
            for g in range(8):
                S.dma("sp", lambda e, g=g: e.dma_start(out=unT[:, g * 4:(g + 1) * 4, :],
                                                      in_=un_scr[g * 4:(g + 1) * 4].rearrange("t p n -> p t n")),
                      reads=[unscrt], writes=[unTt])
                wz, wzt = wslab(w_in[:, OFF_Z + g * 512:OFF_Z + (g + 1) * 512])
                for j in range(4):
                    for (t0, tn) in BLK:
                        pb, pbt = bank()
                        for kt in range(16):
                            mm(pb[:, 0:tn], pbt, wz[:, kt, j * 128:(j + 1) * 128], xnM[:, kt, t0:t0 + tn], [wzt, xnMt],
                               start=(kt == 0), stop=(kt == 15))
                        S.op("act", lambda e, pb=pb, tn=tn: e.activation(out=gth[:, 0:tn], in_=pb[:, 0:tn], func=AF.Tanh, scale=0.5),
                             reads=[pbt], writes=[gtht])
                        S.op("dve", lambda e, pb=pb, tn=tn: e.scalar_tensor_tensor(
                            out=gzz[:, 0:tn], in0=gth[:, 0:tn], scalar=1.0, in1=pb[:, 0:tn], op0=ALU.add, op1=ALU.mult),
                            reads=[gtht, pbt], writes=[gzzt])
                        S.op("dve", lambda e, g=g, j=j, t0=t0, tn=tn: e.tensor_tensor(
                            out=ug[:, j, t0:t0 + tn], in0=unT[:, g * 4 + j, t0:t0 + tn], in1=gzz[:, 0:tn], op=ALU.mult),
                            reads=[unTt, gzzt], writes=[ugt])
                    S.op("act", lambda e, j=j: e.activation(out=u2[:, j, :], in_=ug[:, j, :], func=AF.Square),
                         reads=[ugt], writes=[u2t])
                for (t0, tn) in BLK:
                    pb, pbt = bank()
                    for j in range(4):
                        mm(pb[:, 0:tn], pbt, onesb, u2[:, j, t0:t0 + tn], [onesbt, u2t], start=(j == 0), stop=(j == 3))
                    S.op("act", lambda e, pb=pb, t0=t0, tn=tn: e.activation(
                        out=grs[:, t0:t0 + tn], in_=pb[:, 0:tn], func=AF.Ln, scale=1.0 / 512, bias=float(4 * EPS)),
                        reads=[pbt], writes=[grst])
                S.op("act", lambda e: e.activation(out=grs, in_=grs, func=AF.Exp, scale=-0.5), reads=[grst], writes=[grst])
                for j in range(4):
                    S.op("dve", lambda e, g=g, j=j: e.scalar_tensor_tensor(
                        out=unT[:, g * 4 + j, :], in0=ug[:, j, :], scalar=P("gssm", g * 4 + j, g * 4 + j + 1), in1=grs,
                        op0=ALU.mult, op1=ALU.mult), reads=[ugt, ppt, grst], writes=[unTt])
            S.barrier()
            A.release(mG)
            mixedT, mixedTt = A.alloc("mixedT", [16, NMAIN], BF16)
            tg_, tgt_ = A.alloc("tg", [2, NMAIN], F32)
            mtmp, mtmpt = A.alloc("mtmp", [NMAIN], F32)
            mC = A.mark()
            dump("unT", unT.rearrange("p a b -> p (a b)"), unTt)
            for fp in range(8):
                gate_pair(fp, OFF_G + 2048)
                for rb in range(2):
                    wb, wbt = half_slab(w_out_b[rb * 2048:(rb + 1) * 2048, fp * 256:(fp + 1) * 256])
                    pair_proj(wb, wbt, 16, lambda kt, t0, tn: unT[:, kt, t0:t0 + tn], [unTt], kt0=rb * 16,
                              first=(rb == 0), last=(rb == 1))
                merge_pair(fp, True)
            dump("mixB", mixedT.rearrange("p a b -> p (a b)"), mixedTt)
            S.barrier()
            A.release(mC)
            A.release_top()

        if upto >= 6:
            uaT, uaTt = A.alloc("uaT", [16, NMAIN], BF16)
            CHW = 2 + 1024 + 160
            chb, chbt = A.alloc("chb", [CHW], F32)
            bsb, bsbt = A.alloc("bsb", [NMAIN], F32)
            cva, cvat = A.alloc("cva", [NMAIN], F32)
            tla, tlat = A.alloc("tla", [34], F32)
            capT, capTt = A.alloc("capT", [16, 128], F32)
            sca_in, sca_int = A.alloc("sca_in", [D], F32)
            scaT, scaTt = A.alloc("scaT", [16, 32], F32)
            S.dma("sp", lambda e: e.dma_start(out=sca_in[0:32, :], in_=sca), writes=[sca_int])
            for k4 in range(4):
                pb, pbt = bank(5)
                for k in range(4):
                    f = k4 * 4 + k
                    tr(pb[:, k * 32:(k + 1) * 32], pbt, sca_in[0:32, f * 128:(f + 1) * 128], ident[0:32, 0:32], [sca_int, ppt])
                S.op("act", lambda e, pb=pb, k4=k4: e.copy(out=scaT[:, k4 * 4:(k4 + 1) * 4, :].rearrange("p a b -> p (a b)"),
                                                           in_=pb[:, 0:128]), reads=[pbt], writes=[scaTt])
            wca = P("wca").rearrange("p (t k) -> p t k", k=3)
            for f in range(16):
                b_, bt_ = ring[ring_i[0] % NSLOT]
                ring_i[0] += 1
                w3 = b_.rearrange("p a b -> p (a b)")[:, 0:16 * 384].rearrange("p (a b) -> p a b", a=16)
                for q, off in enumerate((OFF_BA, OFF_CA, OFF_HA)):
                    S.dma("pool", lambda e, q=q, off=off, f=f, w3=w3: e.dma_start(
                        out=w3[:, :, q * 128:(q + 1) * 128],
                        in_=w_in[:, off + f * 128:off + (f + 1) * 128].rearrange("(kt p) f -> p kt f", p=128)), writes=[bt_])
                if True:
                    ph, pht = bank(5)
                    for q in (1, 2):
                        for kt in range(16):
                            mm(ph[:, (q - 1) * 2:(q - 1) * 2 + 2], pht, w3[:, kt, q * 128:(q + 1) * 128],
                               xnP2[:, kt, :], [bt_, xnP2t], start=(kt == 0), stop=(kt == 15))
                    S.op("act", lambda e, ph=ph: e.copy(out=chb[:, 0:2], in_=ph[:, 0:2]), reads=[pht], writes=[chbt])
                    S.op("dve", lambda e, ph=ph: e.tensor_tensor(out=chb[:, 0:2], in0=chb[:, 0:2], in1=ph[:, 2:4], op=ALU.mult),
                         reads=[pht, chbt], writes=[chbt])
                    chs = chb[:, 1026:1186].rearrange("p (b k) -> p b k", k=10)
                    S.op("dve", lambda e, chs=chs, f=f: e.tensor_copy(
                        out=chs[:, :, 0:2], in_=scaT[:, f, :].rearrange("p (b k) -> p b k", k=2)), reads=[scaTt], writes=[chbt])
                    for q in (1, 2, 0):
                        for bi, (t0, tn) in enumerate(BLK):
                            pb, pbt = bank()
                            for kt in range(16):
                                mm(pb[:, 0:tn], pbt, w3[:, kt, q * 128:(q + 1) * 128], xnM[:, kt, t0:t0 + tn],
                                   [bt_, xnMt], start=(kt == 0), stop=(kt == 15))
                            if bi < 2:
                                dst = chb[:, 2 + t0:2 + t0 + 512]
                                src_ = pb[:, 0:512]
                            else:
                                dst = chs[:, :, 2:10]
                                src_ = pb[:, 0:128].rearrange("p (b k) -> p b k", k=8)
                            if q == 1:
                                S.op("act", lambda e, dst=dst, src_=src_: e.copy(out=dst, in_=src_), reads=[pbt], writes=[chbt])
                            elif q == 2:
                                S.op("dve", lambda e, dst=dst, src_=src_: e.tensor_tensor(out=dst, in0=dst, in1=src_, op=ALU.mult),
                                     reads=[pbt, chbt], writes=[chbt])
                            else:
                                S.op("act", lambda e, pb=pb, t0=t0, tn=tn: e.copy(out=bsb[:, t0:t0 + tn], in_=pb[:, 0:tn]),
                                     reads=[pbt], writes=[bsbt])
                    S.op("dve", lambda e: e.tensor_copy(out=tla[:, 0:2], in_=chb[:, 1024:1026]), reads=[chbt], writes=[tlat])
                    S.op("dve", lambda e, chs=chs: e.tensor_copy(out=tla[:, 2:34].rearrange("p (b k) -> p b k", k=2),
                                                                  in_=chs[:, :, 8:10]), reads=[chbt], writes=[tlat])
                    pt_, ptt_ = bank(5)
                    tr(pt_[0:34, 0:128], ptt_, tla, ident, [tlat, ppt])
                    S.op("act", lambda e, pt_=pt_, f=f: e.copy(out=capT[0:34, f, :], in_=pt_[0:34, 0:128]), reads=[ptt_], writes=[capTt])
                    S.op("act", lambda e, f=f: e.activation(out=cva[:, 0:1024], in_=chb[:, 2:1026], func=AF.Identity,
                                                             scale=wca[:, f, 2:3]), reads=[chbt, ppt], writes=[cvat])
                    cvs_ = cva[:, 1024:1152].rearrange("p (b k) -> p b k", k=8)
                    S.op("act", lambda e, f=f, chs=chs, cvs_=cvs_: e.activation(out=cvs_, in_=chs[:, :, 2:10], func=AF.Identity,
                                                                               scale=wca[:, f, 2:3]), reads=[chbt, ppt], writes=[cvat])
                    for k in range(2):
                        S.op("dve", lambda e, f=f, k=k: e.scalar_tensor_tensor(
                            out=cva[:, 0:1024], in0=chb[:, k:k + 1024], scalar=wca[:, f, k:k + 1], in1=cva[:, 0:1024],
                            op0=ALU.mult, op1=ALU.add), reads=[chbt, ppt, cvat], writes=[cvat])
                        S.op("dve", lambda e, f=f, k=k, chs=chs, cvs_=cvs_: e.scalar_tensor_tensor(
                            out=cvs_, in0=chs[:, :, k:k + 8], scalar=wca[:, f, k:k + 1], in1=cvs_,
                            op0=ALU.mult, op1=ALU.add), reads=[chbt, ppt, cvat], writes=[cvat])
                    S.op("dve", lambda e, f=f: e.tensor_tensor(out=uaT[:, f, :], in0=bsb, in1=cva, op=ALU.mult),
                         reads=[bsbt, cvat], writes=[uaTt])
            S.dma("sp", lambda e: e.dma_start(out=cap_o, in_=capT[0:2].rearrange("p a b -> p (a b)")), reads=[capTt], tok=capTt)
            S.dma("sp", lambda e: e.dma_start(out=cas_o.rearrange("b k f -> (b k) f"), in_=capT[2:34].rearrange("p a b -> p (a b)")),
                  reads=[capTt], tok=capTt)
            dump("uaT", uaT.rearrange("p a b -> p (a b)"), uaTt)
            for fp in range(8):
                gate_pair(fp, OFF_G)
                wa, wat = half_slab(w_out_a[:, fp * 256:(fp + 1) * 256])
                pair_proj(wa, wat, 16, lambda kt, t0, tn: uaT[:, kt, t0:t0 + tn], [uaTt])
                merge_pair(fp, False)
            dump("mixA", mixedT.rearrange("p a b -> p (a b)"), mixedTt)
            S.barrier()
            A.release(mC)

        if upto >= 7:
            SC = 128 ** -0.5
            KT, KTt = A.alloc("KT", [4, 256], BF16)
            Vb, Vbt = A.alloc("Vb", [2, 512], BF16)
            qT, qTt = A.alloc("qT", [4, NMAIN], BF16)
            oT, oTt = A.alloc("oT", [4, NMAIN], BF16)
            mD = A.mark()
            memnT, memnTt = A.alloc("memnT", [16, 256], BF16)
            kvo = [A.alloc(f"kvo{i}", [512], F32) for i in range(2)]
            mx_in = [A.alloc(f"m_xin{i}", [D], F32) for i in range(2)]
            mx_s, mx_st = A.alloc("m_xs", [D], BF16)
            mj, mjt = A.alloc("m_junk", [D], BF16)
            msq, msqt = A.alloc("m_ssq", [2], F32)
            mr, mrt = A.alloc("m_r", [2], F32)
            for c in range(2):
                xi, xit = mx_in[c]
                S.dma("sp", lambda e, xi=xi, c=c: e.dma_start(out=xi, in_=memp[c * 128:(c + 1) * 128, :]), writes=[xit])
                S.op("act", lambda e, xi=xi, c=c: e.activation(out=mj, in_=xi, func=AF.Square, accum_out=msq[:, c:c + 1]),
                     reads=[xit], writes=[mjt, msqt])
            rstd_of(mr, mrt, msq, msqt, D, EPS)
            for c in range(2):
                xi, xit = mx_in[c]
                S.op("dve", lambda e, xi=xi, c=c: e.tensor_scalar(out=mx_s, in0=xi, scalar1=mr[:, c:c + 1], scalar2=None, op0=ALU.mult),
                     reads=[xit, mrt], writes=[mx_st])
                for half in range(2):
                    pb, pbt = bank()
                    pbb = pb.bitcast(BF16).rearrange("p (a b) -> p a b", a=8)
                    for k in range(8):
                        kt = half * 8 + k
                        tr(pbb[:, k, :], pbt, mx_s[:, kt * 128:(kt + 1) * 128], identb, [mx_st, identbt])
                    for k in range(8):
                        kt = half * 8 + k
                        S.op("dve" if k % 2 else "act", (lambda e, pbb=pbb, k=k, kt=kt, c=c: e.tensor_scalar(
                            out=memnT[:, kt, c * 128:(c + 1) * 128], in0=pbb[:, k, :], scalar1=P("gmem", kt, kt + 1), scalar2=None,
                            op0=ALU.mult)) if k % 2 else (lambda e, pbb=pbb, k=k, kt=kt, c=c: e.activation(
                                out=memnT[:, kt, c * 128:(c + 1) * 128], in_=pbb[:, k, :], func=AF.Copy, scale=P("gmem", kt, kt + 1))),
                            reads=[pbt, ppt], writes=[memnTt])
            wk, wkt = wslab(w_mem_kv[:, 0:512])
            for h in range(4):
                pb, pbt = bank()
                for kt in range(16):
                    mm(pb[:, 0:256], pbt, wk[:, kt, h * 128:(h + 1) * 128], memnT[:, kt, :], [wkt, memnTt], start=(kt == 0), stop=(kt == 15))
                S.op("act", lambda e, pb=pb, h=h: e.copy(out=KT[:, h, :], in_=pb[:, 0:256]), reads=[pbt], writes=[KTt])
            for mt in range(2):
                pb, pbt = bank()
                for kt in range(16):
                    mm(pb, pbt, memnT[:, kt, mt * 128:(mt + 1) * 128], wk[:, kt, :], [wkt, memnTt], start=(kt == 0), stop=(kt == 15))
                ko, kot = kvo[mt]
                S.op("act", lambda e, pb=pb, ko=ko: e.copy(out=ko, in_=pb), reads=[pbt], writes=[kot])
                S.dma("sp", lambda e, ko=ko, mt=mt: e.dma_start(out=memk_o[mt * 128:(mt + 1) * 128, :], in_=ko), reads=[kot], tok=kot)
            wv, wvt = wslab(w_mem_kv[:, 512:1024])
            for mt in range(2):
                pb, pbt = bank()
                for kt in range(16):
                    mm(pb, pbt, memnT[:, kt, mt * 128:(mt + 1) * 128], wv[:, kt, :], [wvt, memnTt], start=(kt == 0), stop=(kt == 15))
                ko, kot = kvo[mt]
                S.op("act", lambda e, pb=pb, ko=ko: e.copy(out=ko, in_=pb), reads=[pbt], writes=[kot])
                S.op("dve", lambda e, pb=pb, mt=mt: e.tensor_copy(out=Vb[:, mt, :], in_=pb), reads=[pbt], writes=[Vbt])
                S.dma("sp", lambda e, ko=ko, mt=mt: e.dma_start(out=memv_o[mt * 128:(mt + 1) * 128, :], in_=ko), reads=[kot], tok=kot)
            wq, wqt = wslab(w_in[:, OFF_Q:OFF_Q + 512])
            for h in range(4):
                for (t0, tn) in BLK:
                    pb, pbt = bank()
                    for kt in range(16):
                        mm(pb[:, 0:tn], pbt, wq[:, kt, h * 128:(h + 1) * 128], xnM[:, kt, t0:t0 + tn], [wqt, xnMt],
                           start=(kt == 0), stop=(kt == 15))
                    S.op("act", lambda e, pb=pb, h=h, t0=t0, tn=tn: e.copy(out=qT[:, h, t0:t0 + tn], in_=pb[:, 0:tn]),
                         reads=[pbt], writes=[qTt])
            S.barrier()
            A.release(mD)
            mxs, mxst = A.alloc("a_mx", [4], F32)
            nb_, nbt_ = A.alloc("a_nb", [4], F32)
            ssum, ssumt = A.alloc("a_ss", [4], F32)
            rsum, rsumt = A.alloc("a_rs", [4], F32)
            ex_ = [A.alloc(f"a_e{i}", [256], F32) for i in range(2)]
            pn = [A.alloc(f"a_pn{i}", [4, 256], BF16) for i in range(2)]
            pT = [A.alloc(f"a_pT{i}", [8, 128], BF16) for i in range(2)]

            def softmax_rows(ps_list, k2):
                p_, pt_ = pn[k2]
                for h, (ps, pst_) in enumerate(ps_list):
                    S.op("dve", lambda e, ps=ps, h=h: e.reduce_max(out=mxs[:, h:h + 1], in_=ps, axis=AX.X), reads=[pst_], writes=[mxst])
                S.op("dve", lambda e: e.tensor_scalar(out=nb_, in0=mxs, scalar1=-SC, scalar2=None, op0=ALU.mult), reads=[mxst], writes=[nbt_])
                for h, (ps, pst_) in enumerate(ps_list):
                    ex, ext = ex_[h % 2]
                    S.op("act", lambda e, ps=ps, h=h, ex=ex: e.activation(out=ex, in_=ps, func=AF.Exp, scale=SC, bias=nb_[:, h:h + 1],
                                                                         accum_out=ssum[:, h:h + 1]), reads=[pst_, nbt_], writes=[ext, ssumt])
                    S.op("dve", lambda e, h=h: e.reciprocal(out=rsum[:, h:h + 1], in_=ssum[:, h:h + 1]), reads=[ssumt], writes=[rsumt])
                    S.op("dve", lambda e, h=h, ex=ex, p_=p_: e.tensor_scalar(out=p_[:, h, :], in0=ex, scalar1=rsum[:, h:h + 1], scalar2=None,
                                                                            op0=ALU.mult), reads=[ext, rsumt], writes=[pt_])
                return p_, pt_

            for c in range(8):
                k2 = c % 2
                ps_list = []
                for h2 in range(2):
                    pb, pbt = bank(4 + h2)
                    for hh in range(2):
                        h = h2 * 2 + hh
                        mm(pb[:, hh * 256:(hh + 1) * 256], pbt, qT[:, h, c * 128:(c + 1) * 128], KT[:, h, :], [qTt, KTt])
                        ps_list.append((pb[:, hh * 256:(hh + 1) * 256], pbt))
                p_, pt_ = softmax_rows(ps_list, k2)
                ptr_, ptrt_ = bank()
                ptb = ptr_.bitcast(BF16).rearrange("p (a b) -> p a b", a=8)
                for h in range(4):
                    for mt in range(2):
                        tr(ptb[:, h * 2 + mt, :], ptrt_, p_[:, h, mt * 128:(mt + 1) * 128], identb, [pt_, identbt])
                pt2, pt2t = pT[k2]
                S.op("act", lambda e, pt2=pt2, ptb=ptb: e.copy(out=pt2, in_=ptb), reads=[ptrt_], writes=[pt2t])
                po_, pot_ = bank()
                for h in range(4):
                    for mt in range(2):
                        mm(po_[:, h * 128:(h + 1) * 128], pot_, Vb[:, mt, h * 128:(h + 1) * 128], pt2[:, h * 2 + mt, :], [Vbt, pt2t],
                           start=(mt == 0), stop=(mt == 1))
                S.op("dve", lambda e, po_=po_, c=c: e.tensor_copy(out=oT[:, :, c * 128:(c + 1) * 128],
                                                                  in_=po_.rearrange("p (h l) -> p h l", h=4)), reads=[pot_], writes=[oTt])
            kcb, kcbt = A.alloc("kcb", [16, 2, 128], BF16)
            vcb, vcbt = A.alloc("vcb", [16, 2, 128], BF16)
            KsT, KsTt = A.alloc("KsT", [16, 256], BF16)
            qTm, qTmt = A.alloc("qTm", [16, 128], BF16)
            S.op("dve", lambda e: e.memset(qTm, 0.0), writes=[qTmt])
            kc_v = kc.rearrange("b (mt p) (hd d) -> p b mt hd d", p=128, d=128)
            vc_v = vc.rearrange("b (mt p) (hd d) -> p b mt hd d", p=128, d=128)
            for h in range(4):
                for b in range(16):
                    S.dma("pool", lambda e, h=h, b=b: e.dma_start(out=kcb[:, b, :, :], in_=kc_v[:, b, :, h, :]), writes=[kcbt])
                    S.dma("pool", lambda e, h=h, b=b: e.dma_start(out=vcb[:, b, :, :], in_=vc_v[:, b, :, h, :]), writes=[vcbt])
                for b4 in range(4):
                    ptr_, ptrt_ = bank()
                    ptb = ptr_.bitcast(BF16).rearrange("p (a b) -> p a b", a=8)
                    for bb in range(4):
                        for mt in range(2):
                            tr(ptb[:, bb * 2 + mt, :], ptrt_, kcb[:, b4 * 4 + bb, mt, :], identb, [kcbt, identbt])
                    S.op("act" if b4 % 2 else "dve", (lambda e, ptb=ptb, b4=b4: e.copy(
                        out=KsT[:, b4 * 4:(b4 + 1) * 4, :].rearrange("p a b -> p (a b)"), in_=ptb.rearrange("p a b -> p (a b)")))
                        if b4 % 2 else (lambda e, ptb=ptb, b4=b4: e.tensor_copy(
                            out=KsT[:, b4 * 4:(b4 + 1) * 4, :].rearrange("p a b -> p (a b)"), in_=ptb.rearrange("p a b -> p (a b)"))),
                        reads=[ptrt_], writes=[KsTt])
                for b in range(16):
                    S.op("dve", lambda e, h=h, b=b: e.tensor_copy(out=qTm[:, b, 8 * b:8 * b + 8],
                                                                   in_=qT[:, h, 1024 + 8 * b:1024 + 8 * b + 8]), reads=[qTt], writes=[qTmt])
                pb, pbt = bank(4)
                for b in range(16):
                    mm(pb[:, 0:256], pbt, qTm[:, b, :], KsT[:, b, :], [qTmt, KsTt], start=(b == 0), stop=(b == 15))
                p_, pt_ = softmax_rows([(pb[:, 0:256], pbt)], h % 2)
                ptr_, ptrt_ = bank()
                ptb = ptr_.bitcast(BF16).rearrange("p (a b) -> p a b", a=8)
                for mt in range(2):
                    tr(ptb[:, mt, :], ptrt_, p_[:, 0, mt * 128:(mt + 1) * 128], identb, [pt_, identbt])
                pt2, pt2t = pT[h % 2]
                S.op("act", lambda e, pt2=pt2, ptb=ptb: e.copy(out=pt2[:, 0:2, :], in_=ptb[:, 0:2, :]), reads=[ptrt_], writes=[pt2t])
                po_, pot_ = bank(5)
                for b in range(16):
                    for mt in range(2):
                        mm(po_[:, 8 * b:8 * b + 8], pot_, vcb[:, b, mt, :], pt2[:, mt, 8 * b:8 * b + 8], [vcbt, pt2t],
                           start=(mt == 0), stop=(mt == 1))
                S.op("dve", lambda e, po_=po_, h=h: e.tensor_copy(out=oT[:, h, 1024:1152], in_=po_[:, 0:128]), reads=[pot_], writes=[oTt])
            dump("oT", oT.rearrange("p a b -> p (a b)"), oTt)
            wxo, bt_ = A.alloc("wxo", [4, D], BF16)
            S.dma("pool", lambda e: e.dma_start(out=wxo, in_=w_out_x.rearrange("(kt p) f -> p kt f", p=128)), writes=[bt_])
            for fp in range(8):
                gate_pair(fp, OFF_G + 4096)
                pair_proj(wxo[:, :, fp * 256:(fp + 1) * 256], bt_, 4, lambda kt, t0, tn: oT[:, kt, t0:t0 + tn], [oTt])
                merge_pair(fp, False)
            dump("mixX", mixedT.rearrange("p a b -> p (a b)"), mixedTt)
            S.op("dve", lambda e: e.tensor_copy(out=xnM, in_=mixedT), reads=[mixedTt], writes=[xnMt])
            S.barrier()
            A.release(mP0)

        if upto >= 8:
            mixT, mixTt = xnM, xnMt
            gpo, gpot = A.alloc("gpost_b", [D], F32)
            gmo, gmot = A.alloc("gmpost_b", [D], F32)
            S.dma("sp", lambda e: e.dma_start(out=gpo, in_=g_post.partition_broadcast(128).rearrange("p a d -> p (a d)")), writes=[gpot])
            S.dma("sp", lambda e: e.dma_start(out=gmo, in_=g_mpost.partition_broadcast(128).rearrange("p a d -> p (a d)")), writes=[gmot])
            x2 = [A.alloc(f"x2_{i}", [D], F32) for i in range(3)]
            ffa, ffat = A.alloc("ffo", [3, D], F32)
            ffo = [(ffa[:, i, :], ffat) for i in range(3)]
            xl = [A.alloc("xl0", [D], F32)] * 2
            hff, hfft = A.alloc("hff", [64, 384], BF16)
            rtmp = [A.alloc(f"rtmp{i}", [384], F32) for i in range(2)]
            hnb, hnbt = A.alloc("hnb", [D], BF16)
            sq8, sq8t = A.alloc("sq8", [4], F32)
            r8, r8t = A.alloc("r8", [4], F32)
            junk8, junk8t = hnb, hnbt
            hnT = ffa.rearrange("p a b -> p (a b)").bitcast(BF16)[:, 0:16 * 384].rearrange("p (a b) -> p a b", a=16)
            hnTt = ffat
            ctoks = [Tok(f"wc{i}") for i in range(36)]

            def p8slab(idx, src_ap, tg3):
                if tg3 == 0:
                    v, t = wslab(src_ap)
                    S.dma("sp", lambda e: e.dma_start(out=wcache[idx], in_=v.rearrange("p a b -> p (a b)")), reads=[t],
                          writes=[ctoks[idx]], tok=t)
                    return v, t
                b_, bt_ = ring[ring_i[0] % NSLOT]
                ring_i[0] += 1
                S.dma("sp", lambda e: e.dma_start(out=b_.rearrange("p a b -> p (a b)"), in_=wcache[idx]), reads=[ctoks[idx]],
                      writes=[bt_])
                return b_, bt_

            for tg3 in range(3):
                ch0 = tg3 * 3
                for cb4 in range(4):
                    wo, wot = p8slab(cb4, w_o[:, cb4 * 512:(cb4 + 1) * 512], tg3)
                    for ci in range(3):
                        c = ch0 + ci
                        pb, pbt = bank()
                        for kt in range(16):
                            mm(pb, pbt, mixT[:, kt, c * 128:(c + 1) * 128], wo[:, kt, :], [mixTt, wot], start=(kt == 0), stop=(kt == 15))
                        xx, xxt = x2[ci]
                        S.op("act", lambda e, pb=pb, xx=xx, cb4=cb4: e.copy(out=xx[:, cb4 * 512:(cb4 + 1) * 512], in_=pb),
                             reads=[pbt], writes=[xxt])
                for ci in range(3):
                    c = ch0 + ci
                    xx, xxt = x2[ci]
                    xl_, xlt_ = xl[ci % 2]
                    S.dma("sp", lambda e, xl_=xl_, c=c: e.dma_start(out=xl_, in_=xall[NPREV + c * 128:NPREV + (c + 1) * 128, :]),
                          writes=[xlt_])
                    S.op("act", lambda e, xx=xx: e.activation(out=junk8, in_=xx, func=AF.Square, accum_out=sq8[:, 0:1]),
                         reads=[xxt], writes=[junk8t, sq8t])
                    rstd_of(r8[:, 0:1], r8t, sq8[:, 0:1], sq8t, D, EPS / 4)
                    S.op("dve", lambda e, xx=xx: e.scalar_tensor_tensor(out=xx, in0=xx, scalar=r8[:, 0:1], in1=gpo,
                                                                        op0=ALU.mult, op1=ALU.mult), reads=[xxt, r8t, gpot], writes=[xxt])
                    S.op("dve", lambda e, xx=xx, xl_=xl_: e.tensor_tensor(out=xx, in0=xx, in1=xl_, op=ALU.add),
                         reads=[xxt, xlt_], writes=[xxt])
                    S.op("act", lambda e, xx=xx: e.activation(out=junk8, in_=xx, func=AF.Square, accum_out=sq8[:, 1:2]),
                         reads=[xxt], writes=[junk8t, sq8t])
                    rstd_of(r8[:, 1:2], r8t, sq8[:, 1:2], sq8t, D, EPS)
                    S.op("dve", lambda e, xx=xx: e.tensor_scalar(out=hnb, in0=xx, scalar1=r8[:, 1:2], scalar2=None, op0=ALU.mult),
                         reads=[xxt, r8t], writes=[hnbt])
                    for half in range(2):
                        pb, pbt = bank()
                        pbb = pb.bitcast(BF16).rearrange("p (a b) -> p a b", a=8)
                        for k in range(8):
                            kt = half * 8 + k
                            tr(pbb[:, k, :], pbt, hnb[:, kt * 128:(kt + 1) * 128], identb, [hnbt, identbt])
                        for k in range(8):
                            kt = half * 8 + k
                            if k % 2:
                                S.op("dve", lambda e, pbb=pbb, k=k, kt=kt, ci=ci: e.tensor_scalar(
                                    out=hnT[:, kt, ci * 128:(ci + 1) * 128], in0=pbb[:, k, :], scalar1=P("gmlp", kt, kt + 1), scalar2=None,
                                    op0=ALU.mult), reads=[pbt, ppt], writes=[hnTt])
                            else:
                                S.op("act", lambda e, pbb=pbb, k=k, kt=kt, ci=ci: e.activation(
                                    out=hnT[:, kt, ci * 128:(ci + 1) * 128], in_=pbb[:, k, :], func=AF.Copy, scale=P("gmlp", kt, kt + 1)),
                                    reads=[pbt, ppt], writes=[hnTt])
                for s in range(16):
                    w1, w1t = p8slab(4 + s, w_ff1[:, s * 512:(s + 1) * 512], tg3)
                    for j in range(4):
                        pb, pbt = bank()
                        for kt in range(16):
                            mm(pb[:, 0:384], pbt, w1[:, kt, j * 128:(j + 1) * 128], hnT[:, kt, :], [w1t, hnTt], start=(kt == 0), stop=(kt == 15))
                        rt_, rtt_ = rtmp[j % 2]
                        S.op("act", lambda e, pb=pb, rt_=rt_: e.activation(out=rt_, in_=pb[:, 0:384], func=AF.Relu), reads=[pbt], writes=[rtt_])
                        S.op("dve", lambda e, rt_=rt_, s=s, j=j: e.tensor_tensor(out=hff[:, s * 4 + j, :], in0=rt_, in1=rt_, op=ALU.mult),
                             reads=[rtt_], writes=[hfft])
                for cb4 in range(4):
                    for rb in range(4):
                        w2, w2t = p8slab(20 + cb4 * 4 + rb, w_ff2[rb * 2048:(rb + 1) * 2048, cb4 * 512:(cb4 + 1) * 512], tg3)
                        for ci in range(3):
                            pb, pbt = banks[5 + ci][0][:, :], banks[5 + ci][1]
                            for kt in range(16):
                                mm(pb, pbt, hff[:, rb * 16 + kt, ci * 128:(ci + 1) * 128], w2[:, kt, :], [hfft, w2t],
                                   start=(rb == 0 and kt == 0), stop=(rb == 3 and kt == 15))
                    for ci in range(3):
                        pb, pbt = banks[5 + ci][0][:, :], banks[5 + ci][1]
                        fo, fot = ffo[ci]
                        S.op("act", lambda e, pb=pb, fo=fo, cb4=cb4: e.copy(out=fo[:, cb4 * 512:(cb4 + 1) * 512], in_=pb),
                             reads=[pbt], writes=[fot])
                for ci in range(3):
                    c = ch0 + ci
                    xx, xxt = x2[ci]
                    fo, fot = ffo[ci]
                    S.op("act", lambda e, fo=fo: e.activation(out=junk8, in_=fo, func=AF.Square, accum_out=sq8[:, 2:3]),
                         reads=[fot], writes=[junk8t, sq8t])
                    rstd_of(r8[:, 2:3], r8t, sq8[:, 2:3], sq8t, D, EPS)
                    S.op("dve", lambda e, fo=fo: e.scalar_tensor_tensor(out=fo, in0=fo, scalar=r8[:, 2:3], in1=gmo,
                                                                        op0=ALU.mult, op1=ALU.mult), reads=[fot, r8t, gmot], writes=[fot])
                    S.op("dve", lambda e, fo=fo, xx=xx: e.tensor_tensor(out=fo, in0=fo, in1=xx, op=ALU.add),
                         reads=[fot, xxt], writes=[fot])
                    S.dma("sp", lambda e, fo=fo, c=c: e.dma_start(out=y_o[c * 128:(c + 1) * 128, :], in_=fo), reads=[fot], tok=fot)

        S.final_wait("sp")
        S.emit()
    return nc


def make_in_maps(inp):
    f32 = lambda a: np.ascontiguousarray(np.asarray(a, dtype=np.float32))
    xp, xs = f32(inp["x_prompt"]), f32(inp["x_sample"])
    shared = {
        "w_in": f32(inp["w_in"][0]), "w_out_a": f32(inp["w_out_a"][0]), "w_out_b": f32(inp["w_out_b"][0]),
        "w_mem_kv": f32(inp["w_mem_kv"][0]), "w_out_x": f32(inp["w_out_x"][0]), "w_o": f32(inp["w_o"][0]),
        "w_ff1": f32(inp["w_ff1"][0]), "w_ff2": f32(inp["w_ff2"][0]),
        "g_pre": f32(inp["norm_mix_pre"][0]).reshape(1, D), "g_post": f32(inp["norm_mix_post"][0]).reshape(1, D),
        "g_mpost": f32(inp["norm_mlp_post"][0]).reshape(1, D),
    }
    maps = []
    for c in range(NCORES):
        b, half = c // 2, c % 2
        sl = slice(16 * c, 16 * (c + 1))
        xall = np.zeros((NTOK, D), np.float32)
        if half == 1:
            xall[0:NPREV] = xp[b, 0:1024]
        xall[NPREV:NPREV + 1024] = xp[b, half * 1024:(half + 1) * 1024]
        xall[NPREV + 1024:] = xs[sl].reshape(128, D)
        m = dict(shared)
        m["xall"] = xall
        m["pp"] = build_pp(float(half), f32(inp["norm_mlp_pre"][0]), f32(inp["ssm_norm_w"][0]), f32(inp["d_skip"][0]),
                           f32(inp["conv_a_w"][0]), f32(inp["conv_b_w"][0]), f32(inp["conv_b_bias"][0]),
                           f32(inp["dt_bias"][0]), f32(inp["a_log"][0]), f32(inp["norm_mem"][0]))
        m["memp"] = f32(inp["mem_prompt"][b])
        m["kc"] = f32(inp["cache_mem_k"][0, sl]).reshape(16, 256, 512)
        m["vc"] = f32(inp["cache_mem_v"][0, sl]).reshape(16, 256, 512)
        m["sca"] = f32(inp["state_conv_a"][0, sl]).reshape(32, D)
        m["scb"] = f32(inp["state_conv_b"][0, sl]).reshape(48, 6144)
        m["sssm"] = f32(inp["state_ssm"][0, sl])
        maps.append(m)
    return maps


_NC_CACHE = {}


def kernel(**inputs):
    if "nc" not in _NC_CACHE:
        _NC_CACHE["nc"] = build_nc()
    nc = _NC_CACHE["nc"]
    maps = make_in_maps(inputs)
    res = run_bass_kernel_spmd(nc, maps, core_ids=list(range(NCORES)))
    r = res.results
    y_p = np.zeros((4, 2048, D), np.float32)
    y_s = np.zeros((128, 8, D), np.float32)
    mk = np.zeros((1, 4, 256, 4, 128), np.float32)
    mv = np.zeros((1, 4, 256, 4, 128), np.float32)
    cap = np.zeros((1, 4, 2, D), np.float32)
    cbp = np.zeros((1, 4, 3, 6144), np.float32)
    ssp = np.zeros((1, 4, 64, 64, 128), np.float32)
    cas = np.zeros((1, 128, 2, D), np.float32)
    cbs = np.zeros((1, 128, 3, 6144), np.float32)
    sss = np.zeros((1, 128, 64, 64, 128), np.float32)
    for c in range(NCORES):
        b, half = c // 2, c % 2
        sl = slice(16 * c, 16 * (c + 1))
        y = r[c]["y"]
        y_p[b, half * 1024:(half + 1) * 1024] = y[0:1024]
        y_s[sl] = y[1024:].reshape(16, 8, D)
        cas[0, sl] = r[c]["cas"]
        cbs[0, sl] = r[c]["cbs"]
        sss[0, sl] = r[c]["sss"]
        if half == 1:
            mk[0, b] = r[c]["memk"].reshape(256, 4, 128)
            mv[0, b] = r[c]["memv"].reshape(256, 4, 128)
            cap[0, b] = r[c]["cap"]
            cbp[0, b] = r[c]["cbp"]
            ssp[0, b] = r[c]["ssp"]
    return (y_p, y_s, mk, mv, cap, cbp, ssp, cas, cbs, sss)
```

```python
import numpy as np
from contextlib import ExitStack
import concourse.bass as bass
import concourse.mybir as mybir
from concourse.bass_utils import run_bass_kernel_spmd

F32 = mybir.dt.float32
BF16 = mybir.dt.bfloat16
ALU = mybir.AluOpType
AF = mybir.ActivationFunctionType
AX = mybir.AxisListType

NCORES = 8
D = 2048
NPREV = 1024
NMAIN = 1152
NTOK = NPREV + NMAIN
NCH_PREV = 8
NCH = 17
D_IN_PROJ = 23104
OFF_BA, OFF_CA, OFF_HA, OFF_Z, OFF_XBC, OFF_DT, OFF_Q, OFF_G = 0, 2048, 4096, 6144, 10240, 16384, 16448, 16960
EPS = 1e-6
NEG = -30000.0


class Tok:
    __slots__ = ("name", "w", "r", "dsem", "dcount", "excl")

    def __init__(self, name, excl=False):
        self.name = name
        self.excl = excl
        self.w = None
        self.r = []
        self.dsem = None
        self.dcount = 0


class Sched:
    ENGS = ("pe", "dve", "act", "pool", "sp")

    def __init__(self, nc, stack):
        self.nc = nc
        self.stack = stack
        self.ops = {e: [] for e in self.ENGS}
        self.sem = {}
        self.cnt = {e: 0 for e in self.ENGS}
        self.seen = {e: {} for e in self.ENGS}
        for e in ("pe", "dve", "act", "pool"):
            self.sem[e] = stack.enter_context(nc.semaphore("c_" + e))
        self.dma_toks = []
        self.free_dsems = []

    def _need(self, eng, ev):
        sem, val = ev
        k = id(sem)
        if self.seen[eng].get(k, 0) >= val:
            return
        self.seen[eng][k] = val
        self.ops[eng].append(("wait", sem, val))

    def _deps(self, eng, reads, writes):
        need = {}

        def add(ev):
            if ev is None:
                return
            k = id(ev[0])
            if k not in need or need[k][1] < ev[1]:
                need[k] = ev
        for t in reads:
            add(t.w)
        for t in writes:
            add(t.w)
            for ev in t.r:
                add(ev)
        for ev in need.values():
            if eng == "pe" and ev[0] is self.sem["pe"]:
                continue
            self._need(eng, ev)

    def _commit(self, ev, reads, writes):
        for t in writes:
            t.w = ev
            t.r = []
        for t in reads:
            if t not in writes:
                t.r.append(ev)
                if len(t.r) > 24:
                    best = {}
                    for e2 in t.r:
                        k = id(e2[0])
                        if k not in best or best[k][1] < e2[1]:
                            best[k] = e2
                    t.r = list(best.values())

    def op(self, eng, fn, reads=(), writes=()):
        ex = [t for t in reads if t.excl and t not in writes]
        if ex:
            reads = [t for t in reads if not t.excl]
            writes = list(writes) + ex
        self._deps(eng, reads, writes)
        self.cnt[eng] += 1
        ev = (self.sem[eng], self.cnt[eng])
        self.ops[eng].append(("op", fn, self.sem[eng]))
        self._commit(ev, reads, writes)
        return ev

    def dma(self, eng, fn, reads=(), writes=(), tok=None):
        self._deps(eng, reads, writes)
        if tok is None:
            tok = (list(writes) + list(reads))[0]
        if tok.dsem is None:
            tok.dsem = self.stack.enter_context(self.nc.semaphore("d_" + tok.name))
            self.dma_toks.append(tok)
        tok.dcount += 16
        ev = (tok.dsem, tok.dcount)
        self.ops[eng].append(("dma", fn, tok.dsem))
        self._commit(ev, reads, writes)
        return ev

    def all_events(self):
        evs = [(self.sem[e], self.cnt[e]) for e in ("pe", "dve", "act", "pool") if self.cnt[e]]
        evs += [(t.dsem, t.dcount) for t in self.dma_toks if t.dcount]
        return evs

    def barrier(self):
        evs = self.all_events()
        for e in self.ENGS:
            for ev in evs:
                if e == "pe" and ev[0] is self.sem["pe"]:
                    continue
                self._need(e, ev)

    def final_wait(self, eng="sp"):
        for ev in self.all_events():
            self._need(eng, ev)

    def emit(self):
        nc = self.nc
        handles = {"pe": "tensor", "dve": "vector", "act": "scalar", "pool": "gpsimd", "sp": "sync"}
        with nc.Block() as block:
            for e in self.ENGS:
                lst = self.ops[e]
                if not lst:
                    continue

                def body(engine, lst=lst):
                    for item in lst:
                        if item[0] == "wait":
                            engine.wait_ge(item[1], item[2])
                        elif item[0] == "op":
                            item[1](engine).then_inc(item[2], 1)
                        else:
                            item[1](engine).then_inc(item[2], 16)
                getattr(block, handles[e])(body)


class Arena:
    def __init__(self, nc, stack, nbytes):
        self.t = stack.enter_context(nc.sbuf_tensor("arena", [128, nbytes // 4], F32))
        self.top = 0
        self.hi_free = nbytes
        self.nbytes = nbytes

    def alloc(self, name, shape, dtype, top=False):
        esz = 2 if dtype == BF16 else 4
        ne = int(np.prod(shape))
        n4 = (ne * esz + 63) // 64 * 64
        assert self.top + n4 <= self.hi_free, ("SBUF arena overflow", name, self.top, n4, self.hi_free)
        if top:
            self.hi_free -= n4
            off = self.hi_free
        else:
            off = self.top
            self.top += n4
        ap = self.t[:, off // 4:(off + n4) // 4]
        if dtype != F32:
            ap = ap.bitcast(dtype)
        ap = ap[:, 0:ne]
        if len(shape) == 2:
            ap = ap.rearrange("p (a b) -> p a b", a=shape[0])
        elif len(shape) == 3:
            ap = ap.rearrange("p (a b c) -> p a b c", a=shape[0], b=shape[1])
        return ap, Tok(name)

    def mark(self):
        return self.top

    def release(self, m):
        self.top = m

    def release_top(self):
        self.hi_free = self.nbytes


PP_LAYOUT = [("ident", 128), ("triu_p", 128), ("triu_s", 128), ("ones_s", 128), ("negm_p", 128),
             ("negm_s", 128), ("seqmask", 16), ("gmlp", 16), ("gssm", 32), ("drow", 32),
             ("wca", 48), ("wcb", 192), ("bcb", 48), ("dtb", 64), ("alog", 64), ("flag", 1),
             ("gmem", 16)]
PP_OFF = {}
_o = 0
for _n, _w in PP_LAYOUT:
    PP_OFF[_n] = (_o, _w)
    _o += _w
PP_COLS = _o


def build_pp(flag, norm_mlp_pre, ssm_norm_w, d_skip, conv_a_w, conv_b_w, conv_b_bias, dt_bias, a_log, norm_mem):
    pp = np.zeros((128, PP_COLS), np.float32)

    def put(name, arr):
        o, w = PP_OFF[name]
        pp[:, o:o + w] = np.asarray(arr, np.float32).reshape(128, w)
    i = np.arange(128)
    seq = i // 8
    put("ident", np.eye(128))
    put("triu_p", (i[:, None] <= i[None, :]))
    same = seq[:, None] == seq[None, :]
    put("triu_s", (i[:, None] <= i[None, :]) & same)
    put("ones_s", same)
    put("negm_p", np.where(i[:, None] <= i[None, :], 0.0, NEG))
    put("negm_s", np.where((i[:, None] <= i[None, :]) & same, 0.0, NEG))
    put("seqmask", seq[:, None] == np.arange(16)[None, :])
    put("gmlp", norm_mlp_pre.reshape(16, 128).T)
    put("gmem", norm_mem.reshape(16, 128).T)
    put("gssm", ssm_norm_w.reshape(32, 128).T)
    put("drow", d_skip[(2 * np.arange(32)[None, :] + (i[:, None] // 64))])
    put("wca", conv_a_w.reshape(3, 16, 128).transpose(2, 1, 0).reshape(128, 48))
    put("wcb", conv_b_w.reshape(4, 48, 128).transpose(2, 1, 0).reshape(128, 192))
    put("bcb", conv_b_bias.reshape(48, 128).T)
    put("dtb", np.broadcast_to(dt_bias.reshape(1, 64), (128, 64)))
    put("alog", np.broadcast_to(a_log.reshape(1, 64), (128, 64)))
    put("flag", np.full((128, 1), flag))
    return pp


def build_nc(upto=99, dbg=None):
    nc = bass.Bass("TRN2", target_bir_lowering=False)

    def din(name, shape):
        return nc.dram_tensor(name, list(shape), F32, kind="ExternalInput").ap()

    def dout(name, shape):
        return nc.dram_tensor(name, list(shape), F32, kind="ExternalOutput").ap()

    xall = din("xall", [NTOK, D])
    pp_d = din("pp", [128, PP_COLS])
    memp = din("memp", [256, D])
    kc = din("kc", [16, 256, 512])
    vc = din("vc", [16, 256, 512])
    sca = din("sca", [32, D])
    scb = din("scb", [48, 6144])
    sssm = din("sssm", [16, 64, 64, 128])
    w_in = din("w_in", [D, D_IN_PROJ])
    w_out_a = din("w_out_a", [D, D])
    w_out_b = din("w_out_b", [4096, D])
    w_mem_kv = din("w_mem_kv", [D, 1024])
    w_out_x = din("w_out_x", [512, D])
    w_o = din("w_o", [D, D])
    w_ff1 = din("w_ff1", [D, 8192])
    w_ff2 = din("w_ff2", [8192, D])
    g_pre = din("g_pre", [1, D])
    g_post = din("g_post", [1, D])
    g_mpost = din("g_mpost", [1, D])

    y_o = dout("y", [NMAIN, D])
    memk_o = dout("memk", [256, 512])
    memv_o = dout("memv", [256, 512])
    cap_o = dout("cap", [2, D])
    cbp_o = dout("cbp", [3, 6144])
    ssp_o = dout("ssp", [64, 64, 128])
    cas_o = dout("cas", [16, 2, D])
    cbs_o = dout("cbs", [16, 3, 6144])
    sss_o = dout("sss", [16, 64, 64, 128])
    acum_scr = nc.dram_tensor("acum_scr", [9, 64, 128], F32, kind="Internal").ap()
    un_scr = nc.dram_tensor("un_scr", [32, 128, NMAIN], BF16, kind="Internal").ap()
    wcache = nc.dram_tensor("wcache", [36, 128, 8192], BF16, kind="Internal").ap()
    dbg_o = {}
    if dbg:
        for k, shp in dbg.items():
            dbg_o[k] = dout("dbg_" + k, shp)

    with ExitStack() as st:
        S = Sched(nc, st)
        A = Arena(nc, st, 207 * 1024)
        banks = []
        for i in range(8):
            t = st.enter_context(nc.psum_tensor(f"ps{i}", [128, 512], F32))
            banks.append((t, Tok(f"ps{i}", excl=True)))
        bank_i = [0]

        def bank(fixed=None):
            if fixed is None:
                b = banks[bank_i[0] % 4]
                bank_i[0] += 1
            else:
                b = banks[fixed]
            return b[0][:, :], b[1]

        def mm(out, ot, lhsT, rhs, rd, start=True, stop=True, skip=False):
            if skip:
                S.op("pe", lambda e: e.matmul(out, lhsT, rhs, start=start, stop=stop, skip_group_check=True), reads=rd, writes=[ot])
            else:
                S.op("pe", lambda e: e.matmul(out, lhsT, rhs, start=start, stop=stop), reads=rd, writes=[ot])

        def tr(out, ot, in_, ident, rd):
            S.op("pe", lambda e: e.transpose(out, in_, ident), reads=rd, writes=[ot])

        def dump(name, ap, tok):
            if name in dbg_o:
                S.barrier()
                S.dma("pool", lambda e: e.dma_start(out=dbg_o[name], in_=ap), reads=[tok], tok=tok)

        pp, ppt = A.alloc("pp", [PP_COLS], F32)
        S.dma("sp", lambda e: e.dma_start(out=pp, in_=pp_d), writes=[ppt])

        def P(name, a=None, b=None):
            o, w = PP_OFF[name]
            if a is None:
                return pp[:, o:o + w]
            return pp[:, o + a:o + b]
        ident = P("ident")
        identb, identbt = A.alloc("identb", [128], BF16)
        S.op("dve", lambda e: e.tensor_copy(out=identb, in_=ident), reads=[ppt], writes=[identbt])
        wcbh, wcbht = A.alloc("wcbh", [48, 4], F32)
        bcbh, bcbht = A.alloc("bcbh", [48], F32)
        S.op("dve", lambda e: e.tensor_scalar(out=wcbh, in0=P("wcb").rearrange("p (t k) -> p t k", k=4),
                                               scalar1=0.5, scalar2=None, op0=ALU.mult), reads=[ppt], writes=[wcbht])
        S.op("dve", lambda e: e.tensor_scalar(out=bcbh, in0=P("bcb"), scalar1=0.5, scalar2=None, op0=ALU.mult),
             reads=[ppt], writes=[bcbht])
        mhalf, mhalft = A.alloc("mhalf", [1], F32)
        S.op("dve", lambda e: e.memset(mhalf, -0.5), writes=[mhalft])

        NSLOT = 2
        ring = [A.alloc(f"wslab{i}", [16, 512], BF16) for i in range(NSLOT)]
        ring_i = [0]

        def wslab(src_ap, kt=16, ncols=512):
            b, bt = ring[ring_i[0] % NSLOT]
            ring_i[0] += 1
            v = b.rearrange("p a b -> p (a b)")[:, 0:kt * ncols].rearrange("p (a b) -> p a b", a=kt)
            src = src_ap.rearrange("(kt p) f -> p kt f", p=128)
            S.dma("pool", lambda e: e.dma_start(out=v, in_=src), writes=[bt])
            return v, bt

        onesp, onespt = A.alloc("onesp", [128], F32)
        S.op("dve", lambda e: e.memset(onesp, 1.0), writes=[onespt])

        def rstd_of(r, rt, ssq, ssqt, n, eps):
            S.op("act", lambda e: e.activation(out=r, in_=ssq, func=AF.Ln, scale=1.0 / n, bias=float(eps)), reads=[ssqt], writes=[rt])
            S.op("act", lambda e: e.activation(out=r, in_=r, func=AF.Exp, scale=-0.5), reads=[rt], writes=[rt])

        xnM, xnMt = A.alloc("xnM", [16, NMAIN], BF16)
        xnP2, xnP2t = A.alloc("xnP2", [16, 2], BF16)
        mP0 = A.mark()
        dt9, dt9t = A.alloc("dt9", [9, 64], F32)
        ac9, ac9t = A.alloc("ac9", [9, 64], F32)
        dA16, dA16t = A.alloc("dA16", [64], F32)
        wde_t, wde_tt = A.alloc("wde", [NCH, 64], F32)
        cdb_t, cdb_tt = A.alloc("cdb", [NCH, 64], F32)
        hT, hTt = A.alloc("hT", [8, 512], F32)
        hTts = [Tok(f"hT{g}") for g in range(8)]
        halo, halot = A.alloc("halo", [48, 3], BF16)
        mA = A.mark()
        xnP, xnPt = A.alloc("xnP", [16, NPREV], BF16, top=True)
        mB = A.mark()

        gb, gbt = A.alloc("gpre_b", [D], F32)
        S.dma("sp", lambda e: e.dma_start(out=gb, in_=g_pre.partition_broadcast(128).rearrange("p a d -> p (a d)")),
              writes=[gbt])
        n_xin = [A.alloc(f"n_xin{i}", [D], F32) for i in range(2)]
        n_xs = [A.alloc(f"n_xs{i}", [D], BF16) for i in range(2)]
        n_junk, n_junkt = A.alloc("n_junk", [D], BF16)
        n_ssq = [A.alloc(f"n_ssq{i}", [1], F32) for i in range(2)]
        n_r = [A.alloc(f"n_r{i}", [1], F32) for i in range(2)]

        def norm_to_T(src_rows, nchunks, gbc, gbct, dst_of):
            for c in range(nchunks):
                xi, xit = n_xin[c % 2]
                xb, xbt = n_xs[c % 2]
                sq, sqt = n_ssq[c % 2]
                r, rt = n_r[c % 2]
                S.dma("sp", lambda e, xi=xi, c=c: e.dma_start(out=xi, in_=src_rows[c * 128:(c + 1) * 128, :]),
                      writes=[xit])
                S.op("act", lambda e, xi=xi, sq=sq: e.activation(out=n_junk, in_=xi, func=AF.Square, accum_out=sq),
                     reads=[xit], writes=[n_junkt, sqt])
                rstd_of(r, rt, sq, sqt, D, EPS)
                S.op("dve", lambda e, xi=xi, xb=xb, r=r: e.scalar_tensor_tensor(
                    out=xb, in0=xi, scalar=r, in1=gbc, op0=ALU.mult, op1=ALU.mult),
                    reads=[xit, rt, gbct], writes=[xbt])
                for half in range(2):
                    pb, pbt = bank()
                    pbb = pb.bitcast(BF16).rearrange("p (a b) -> p a b", a=8)
                    for k in range(8):
                        kt = half * 8 + k
                        tr(pbb[:, k, :], pbt, xb[:, kt * 128:(kt + 1) * 128], identb, [xbt, identbt])
                    dst, dstt = dst_of(c, half)
                    if half == 0:
                        S.op("act", lambda e, dst=dst, pbb=pbb: e.copy(out=dst, in_=pbb), reads=[pbt], writes=[dstt])
                    else:
                        S.op("dve", lambda e, dst=dst, pbb=pbb: e.tensor_copy(out=dst, in_=pbb), reads=[pbt],
                             writes=[dstt])

        def xn_dst(c, half):
            if c < NCH_PREV:
                return xnP[:, half * 8:half * 8 + 8, c * 128:(c + 1) * 128], xnPt
            c2 = c - NCH_PREV
            return xnM[:, half * 8:half * 8 + 8, c2 * 128:(c2 + 1) * 128], xnMt
        norm_to_T(xall, NCH, gb, gbt, xn_dst)
        S.op("dve", lambda e: e.tensor_copy(out=xnP2, in_=xnP[:, :, NPREV - 2:NPREV]), reads=[xnPt], writes=[xnP2t])
        dump("xnP", xnP.rearrange("p a b -> p (a b)"), xnPt)
        dump("xnM", xnM.rearrange("p a b -> p (a b)"), xnMt)

        def xn_cols(c):
            if c < NCH_PREV:
                return xnP[:, :, c * 128:(c + 1) * 128], xnPt
            c2 = c - NCH_PREV
            return xnM[:, :, c2 * 128:(c2 + 1) * 128], xnMt

        if upto >= 2:
            wdt, wdtt = A.alloc("wdt", [16, 64], BF16)
            S.dma("pool", lambda e: e.dma_start(out=wdt, in_=w_in[:, OFF_DT:OFF_DT + 64].rearrange(
                "(kt p) f -> p kt f", p=128)), writes=[wdtt])
            dt_t, dt_tt = A.alloc("dt", [NCH, 64], F32)
            dA_t, dA_tt = A.alloc("dA", [NCH, 64], F32)
            ac_t, ac_tt = A.alloc("acum", [NCH, 64], F32)
            v_t, v_tt = A.alloc("p2v", [NCH, 64], F32)
            a_t, a_tt = A.alloc("p2a", [NCH, 64], F32)
            tot_t, tot_tt = A.alloc("p2tot", [NCH, 64], F32)
            negA, negAt = A.alloc("negA", [64], F32)
            for c0 in range(0, NCH, 8):
                n = min(8, NCH - c0)
                pb, pbt = bank()
                pv = pb.rearrange("p (a b) -> p a b", b=64)
                for c in range(c0, c0 + n):
                    xc, xct = xn_cols(c)
                    for kt in range(16):
                        mm(pv[:, c - c0, :], pbt, xc[:, kt, :], wdt[:, kt, :], [xct, wdtt], start=(kt == 0), stop=(kt == 15))
                S.op("dve", lambda e, pv=pv, c0=c0, n=n: e.tensor_tensor(
                    out=v_t[:, c0:c0 + n, :], in0=pv[:, 0:n, :],
                    in1=P("dtb").rearrange("p (a b) -> p a b", a=1).to_broadcast([128, n, 64]), op=ALU.add),
                    reads=[pbt, ppt], writes=[v_tt])
            S.op("act", lambda e: e.activation(out=a_t, in_=v_t, func=AF.Abs), reads=[v_tt], writes=[a_tt])
            S.op("act", lambda e: e.activation(out=a_t, in_=a_t, func=AF.Exp, scale=-1.0), reads=[a_tt], writes=[a_tt])
            S.op("act", lambda e: e.activation(out=a_t, in_=a_t, func=AF.Ln, bias=1.0), reads=[a_tt], writes=[a_tt])
            S.op("dve", lambda e: e.scalar_tensor_tensor(out=dt_t, in0=v_t, scalar=0.0, in1=a_t, op0=ALU.max, op1=ALU.add),
                 reads=[v_tt, a_tt], writes=[dt_tt])
            S.op("act", lambda e: e.activation(out=negA, in_=P("alog"), func=AF.Exp), reads=[ppt], writes=[negAt])
            S.op("dve", lambda e: e.tensor_scalar(out=negA, in0=negA, scalar1=-1.0, scalar2=None, op0=ALU.mult),
                 reads=[negAt], writes=[negAt])
            S.op("dve", lambda e: e.tensor_tensor(out=dA_t, in0=dt_t, in1=negA.rearrange("p (a b) -> p a b", a=1)
                                                   .to_broadcast([128, NCH, 64]), op=ALU.mult),
                 reads=[dt_tt, negAt], writes=[dA_tt])
            for (dst, dstt, lp, ls) in ((ac_t, ac_tt, P("triu_p"), P("triu_s")), (tot_t, tot_tt, onesp, P("ones_s"))):
                for c0 in range(0, NCH, 8):
                    n = min(8, NCH - c0)
                    pb, pbt = bank()
                    lhs = ls if c0 == 16 else lp
                    mm(pb[:, 0:n * 64], pbt, lhs, dA_t[:, c0:c0 + n, :].rearrange("p a b -> p (a b)"),
                       [ppt, onespt, dA_tt])
                    S.op("act", lambda e, pb=pb, dst=dst, c0=c0, n=n: e.copy(
                        out=dst[:, c0:c0 + n, :].rearrange("p a b -> p (a b)"), in_=pb[:, 0:n * 64]),
                        reads=[pbt], writes=[dstt])
            S.op("dve", lambda e: e.tensor_tensor(out=wde_t, in0=tot_t, in1=ac_t, op=ALU.subtract),
                 reads=[tot_tt, ac_tt], writes=[wde_tt])
            S.op("act", lambda e: e.activation(out=wde_t, in_=wde_t, func=AF.Exp), reads=[wde_tt], writes=[wde_tt])
            S.op("dve", lambda e: e.tensor_tensor(out=wde_t, in0=wde_t, in1=dt_t, op=ALU.mult),
                 reads=[wde_tt, dt_tt], writes=[wde_tt])
            S.op("act", lambda e: e.activation(out=cdb_t, in_=tot_t, func=AF.Exp), reads=[tot_tt], writes=[cdb_tt])
            acT, acTt = A.alloc("acT", [9, 128], F32)
            scrt = Tok("acum_scr")
            for c0 in range(0, 9, 4):
                n = min(4, 9 - c0)
                pb, pbt = bank()
                for c in range(c0, c0 + n):
                    tr(pb[0:64, (c - c0) * 128:(c - c0 + 1) * 128], pbt, ac_t[:, NCH_PREV + c, :], ident, [ac_tt, ppt])
                S.op("dve", lambda e, pb=pb, c0=c0, n=n: e.tensor_copy(
                    out=acT[0:64, c0:c0 + n, :].rearrange("p a b -> p (a b)"), in_=pb[0:64, 0:n * 128]),
                    reads=[pbt], writes=[acTt])
            S.dma("sp", lambda e: e.dma_start(out=acum_scr.rearrange("c h q -> h c q"), in_=acT[0:64, :, :]),
                  reads=[acTt], writes=[scrt], tok=scrt)
            S.op("dve", lambda e: e.tensor_copy(out=dt9, in_=dt_t[:, NCH_PREV:NCH, :]), reads=[dt_tt], writes=[dt9t])
            S.op("dve", lambda e: e.tensor_copy(out=ac9, in_=ac_t[:, NCH_PREV:NCH, :]), reads=[ac_tt], writes=[ac9t])
            S.op("dve", lambda e: e.tensor_copy(out=dA16, in_=dA_t[:, 16, :]), reads=[dA_tt], writes=[dA16t])
            dump("dt", dt_t.rearrange("p a b -> p (a b)"), dt_tt)
            dump("acum", ac_t.rearrange("p a b -> p (a b)"), ac_tt)
            dump("wde", wde_t.rearrange("p a b -> p (a b)"), wde_tt)
            dump("cdb", cdb_t.rearrange("p a b -> p (a b)"), cdb_tt)
        S.barrier()
        A.release(mB)

        if upto >= 3:
            RAWW = 3 + 1024 + 176
            raws = [A.alloc(f"raw{i}", [6, RAWW], BF16) for i in range(2)]
            cvbufs = [A.alloc(f"cv{i}", [NMAIN], F32) for i in range(2)]
            th, tht = A.alloc("th", [NMAIN], F32)
            gcount = [0]
            xdtd = [A.alloc(f"xdtd{i}", [512], BF16) for i in range(2)]
            btm = [A.alloc(f"btm{i}", [128], BF16) for i in range(2)]
            for i in range(2):
                S.op("dve", lambda e, i=i: e.memset(raws[i][0], 0.0), writes=[raws[i][1]])
            S.op("dve", lambda e: e.memset(hT, 0.0), writes=hTts)
            s2 = {}

            def alloc_s2():
                s2["xdtz"] = [A.alloc(f"xdtz{i}", [8, 128], BF16) for i in range(2)]
                s2["cbT"] = [A.alloc(f"cbT{i}", [128], F32) for i in range(2)]
                s2["acb"] = [A.alloc(f"acb{i}", [8, 128], F32) for i in range(2)]
                s2["ee"] = [A.alloc(f"ee{i}", [8, 128], BF16) for i in range(2)]
                s2["MT"] = [A.alloc(f"MT{i}", [8, 128], BF16) for i in range(2)]
                s2["ea2"] = [A.alloc(f"ea2{i}", [4, 128], F32) for i in range(2)]
                s2["t1"] = [A.alloc(f"t1{i}", [4, 128], F32) for i in range(2)]
                s2["hbf"] = [A.alloc(f"hbf{i}", [512], BF16) for i in range(2)]
                s2["ych"] = [A.alloc(f"ych{i}", [4, 128], BF16) for i in range(2)]
                s2["h0n"] = [A.alloc(f"h0n{i}", [4, 128], F32) for i in range(4)]
                s2["h0T"] = [A.alloc(f"h0T{i}", [512], BF16) for i in range(2)]
                s2["Bm"] = A.alloc("Bm", [16, 128], BF16)
                s2["dAx"] = A.alloc("dAx", [8, 64], F32)
                s2["cdn"] = A.alloc("cdn", [4, 16], F32)
                s2["tailf"] = A.alloc("tailf", [6, 51], F32)
                s2["tailT"] = A.alloc("tailT", [6, 128], F32)
                s2["sc_in"] = (th[:, 0:768].rearrange("p (a b) -> p a b", a=6), tht)
                s2["hnat"] = A.alloc("hnat", [4, 128], F32)
                s2["onesb"] = A.alloc("onesb", [128], BF16)
                S.op("dve", lambda e: e.memset(s2["onesb"][0], 1.0), writes=[s2["onesb"][1]])
                for i in range(2):
                    S.op("dve", lambda e, i=i: e.memset(s2["xdtz"][i][0], 0.0), writes=[s2["xdtz"][i][1]])
            unscrt = Tok("un_scr")
            ssm_in = sssm.rearrange("b (G j hh) p n -> b (hh p) G j n", G=8, j=4, hh=2)
            sss_v = sss_o.rearrange("b (G j hh) p n -> b (hh p) G j n", G=8, j=4, hh=2)
            ssp_v = ssp_o.rearrange("(G j hh) p n -> (hh p) G j n", G=8, j=4, hh=2)
            cbs_v = cbs_o.rearrange("b k f -> (b k) f")
            cnt = [0]

            def tile_global(g, i):
                return g * 4 + i if i < 4 else (32 + g if i == 4 else 40 + g)

            ring4 = []
            for (b_, bt_) in ring:
                flat = b_.rearrange("p a b -> p (a b)")
                for hf in range(2):
                    ring4.append((flat[:, hf * 4096:(hf + 1) * 4096].rearrange("p (a b) -> p a b", a=16), Tok(f"r4_{len(ring4)}")))
            ring4_i = [0]

            def hslab(col0):
                v, t = ring4[ring4_i[0] % 4]
                ring4_i[0] += 1
                S.dma("pool", lambda e: e.dma_start(out=v, in_=w_in[:, col0:col0 + 256].rearrange("(kt p) f -> p kt f", p=128)),
                      writes=[t])
                return v, t

            def ssd_group(g, mode, pump):
                main = mode == "main"
                raw, rawt = raws[gcount[0] % 2]
                gcount[0] += 1

                def X(i, c):
                    return raw[:, i, 3 + c * 128:3 + (c + 1) * 128]
                if main:
                    xdtz, cbT, acb, ee, MT, t1, hbf, ych = (s2[k] for k in ("xdtz", "cbT", "acb", "ee", "MT", "t1", "hbf", "ych"))
                    h0n, h0T, ea2 = s2["h0n"], s2["h0T"], s2["ea2"]
                    (Bm, Bmt) = s2["Bm"]
                    (dAx, dAxt), (cdn, cdnt), (tailf, tailft), (tailT, tailTt) = s2["dAx"], s2["cdn"], s2["tailf"], s2["tailT"]
                    (sc_in, sc_int), (hnat, hnatt), (onesb, onesbt) = s2["sc_in"], s2["hnat"], s2["onesb"]
                wxh = [hslab(OFF_XBC + g * 512 + hf * 256) for hf in range(2)]
                wbc, wbct = ring4[ring4_i[0] % 4]
                ring4_i[0] += 1
                for q, off in ((0, OFF_XBC + 4096 + g * 128), (1, OFF_XBC + 5120 + g * 128)):
                    S.dma("pool", lambda e, q=q, off=off: e.dma_start(
                        out=wbc[:, :, q * 128:(q + 1) * 128],
                        in_=w_in[:, off:off + 128].rearrange("(kt p) f -> p kt f", p=128)), writes=[wbct])
                src, srct = (xnM, xnMt) if main else (xnP, xnPt)
                blocks = [(0, 512), (512, 512)] + ([(1024, 128)] if main else [])
                if main:
                    S.dma("sp", lambda e: e.dma_start(out=sc_in[0:48, 0:4, :],
                                                      in_=scb[:, g * 512:(g + 1) * 512].rearrange("r (t f) -> r t f", t=4)),
                          writes=[sc_int])
                    S.dma("sp", lambda e: e.dma_start(out=sc_in[0:48, 4, :], in_=scb[:, 4096 + g * 128:4096 + (g + 1) * 128]),
                          writes=[sc_int])
                    S.dma("sp", lambda e: e.dma_start(out=sc_in[0:48, 5, :], in_=scb[:, 5120 + g * 128:5120 + (g + 1) * 128]),
                          writes=[sc_int])
                    pb, pbt = bank()
                    for i in range(6):
                        tr(pb[:, i * 48:(i + 1) * 48], pbt, sc_in[0:48, i, :], ident[0:48, 0:48], [sc_int, ppt])
                    for i in range(6):
                        rs = raw[:, i, 1027:1203].rearrange("p (b k) -> p b k", k=11)
                        S.op("act", lambda e, pb=pb, rs=rs, i=i: e.copy(
                            out=rs[:, :, 0:3], in_=pb[:, i * 48:(i + 1) * 48].rearrange("p (b k) -> p b k", k=3)),
                            reads=[pbt], writes=[rawt])
                    yield
                def conv_tile(i):
                    tg = tile_global(g, i)
                    if not main:
                        S.op("dve", lambda e, i=i, tg=tg: e.tensor_copy(out=halo[:, tg, :], in_=raw[:, i, 1024:1027]),
                             reads=[rawt], writes=[halot])
                        if i == 5:
                            return False
                    else:
                        S.op("dve", lambda e, i=i, tg=tg: e.tensor_copy(out=raw[:, i, 0:3], in_=halo[:, tg, :]),
                             reads=[halot], writes=[rawt])
                    cv, cvt = cvbufs[i % 2]
                    S.op("act", lambda e, i=i, tg=tg, cv=cv: e.activation(out=cv[:, 0:1024], in_=raw[:, i, 3:1027], func=AF.Identity,
                                                                           scale=wcbh[:, tg, 3:4], bias=bcbh[:, tg:tg + 1]),
                         reads=[rawt, wcbht, bcbht], writes=[cvt])
                    for k in range(3):
                        S.op("dve", lambda e, i=i, k=k, tg=tg, cv=cv: e.scalar_tensor_tensor(
                            out=cv[:, 0:1024], in0=raw[:, i, k:k + 1024], scalar=wcbh[:, tg, k:k + 1], in1=cv[:, 0:1024],
                            op0=ALU.mult, op1=ALU.add), reads=[rawt, wcbht, cvt], writes=[cvt])
                    W = 1024
                    if main:
                        W = NMAIN
                        rs = raw[:, i, 1027:1203].rearrange("p (b k) -> p b k", k=11)
                        cvs = cv[:, 1024:1152].rearrange("p (b k) -> p b k", k=8)
                        S.op("act", lambda e, rs=rs, cvs=cvs, tg=tg: e.activation(
                            out=cvs, in_=rs[:, :, 3:11], func=AF.Identity, scale=wcbh[:, tg, 3:4], bias=bcbh[:, tg:tg + 1]),
                            reads=[rawt, wcbht, bcbht], writes=[cvt])
                        for k in range(3):
                            S.op("dve", lambda e, rs=rs, cvs=cvs, k=k, tg=tg: e.scalar_tensor_tensor(
                                out=cvs, in0=rs[:, :, k:k + 8], scalar=wcbh[:, tg, k:k + 1], in1=cvs,
                                op0=ALU.mult, op1=ALU.add), reads=[rawt, wcbht, cvt], writes=[cvt])
                    S.op("act", lambda e, W=W, cv=cv: e.activation(out=th[:, 0:W], in_=cv[:, 0:W], func=AF.Tanh),
                         reads=[cvt], writes=[tht])
                    S.op("dve", lambda e, i=i, W=W, cv=cv: e.scalar_tensor_tensor(
                        out=raw[:, i, 3:3 + W], in0=th[:, 0:W], scalar=1.0, in1=cv[:, 0:W], op0=ALU.add, op1=ALU.mult),
                        reads=[tht, cvt], writes=[rawt])
                    return True

                for i in range(6):
                    tg = tile_global(g, i)
                    for (t0, tn) in blocks:
                        pb, pbt = bank()
                        for kt in range(16):
                            lhs = wxh[i // 2][0][:, kt, (i % 2) * 128:(i % 2 + 1) * 128] if i < 4 else wbc[:, kt, (i - 4) * 128:(i - 3) * 128]
                            mm(pb[:, 0:tn], pbt, lhs, src[:, kt, t0:t0 + tn], [wxh[i // 2][1] if i < 4 else wbct, srct],
                               start=(kt == 0), stop=(kt == 15))
                        if t0 < 1024:
                            S.op("act", lambda e, pb=pb, i=i, t0=t0: e.copy(out=raw[:, i, 3 + t0:3 + t0 + 512], in_=pb),
                                 reads=[pbt], writes=[rawt])
                            if main and t0 == 512:
                                S.op("dve", lambda e, pb=pb, i=i: e.tensor_copy(out=tailf[:, i, 0:3], in_=pb[:, 509:512]),
                                     reads=[pbt], writes=[tailft])
                        else:
                            pv = pb[:, 0:128].rearrange("p (b k) -> p b k", k=8)
                            rs = raw[:, i, 1027:1203].rearrange("p (b k) -> p b k", k=11)
                            S.op("act", lambda e, pv=pv, rs=rs: e.copy(out=rs[:, :, 3:11], in_=pv), reads=[pbt], writes=[rawt])
                            S.op("dve", lambda e, pv=pv, i=i: e.tensor_copy(
                                out=tailf[:, i, 3:51].rearrange("p (b k) -> p b k", k=3), in_=pv[:, :, 5:8]),
                                reads=[pbt], writes=[tailft])
                        yield
                    if i >= 1 and conv_tile(i - 1):
                        yield
                if conv_tile(5):
                    yield
                if main:
                    for (i0, n_) in ((0, 4), (4, 2)):
                        pb, pbt = bank()
                        for i in range(i0, i0 + n_):
                            tr(pb[0:51, (i - i0) * 128:(i - i0 + 1) * 128], pbt, tailf[:, i, :], ident, [tailft, ppt])
                        S.op("dve", lambda e, pb=pb, i0=i0, n_=n_: e.tensor_copy(
                            out=tailT[0:51, i0:i0 + n_, :].rearrange("p a b -> p (a b)"), in_=pb[0:51, 0:n_ * 128]),
                            reads=[pbt], writes=[tailTt])
                    for (i0, n_, c0) in ((0, 4, g * 512), (4, 1, 4096 + g * 128), (5, 1, 5120 + g * 128)):
                        S.dma("sp", lambda e, i0=i0, n_=n_, c0=c0: e.dma_start(
                            out=cbp_o[:, c0:c0 + n_ * 128], in_=tailT[0:3, i0:i0 + n_, :].rearrange("p a b -> p (a b)")),
                            reads=[tailTt], tok=tailTt)
                        S.dma("sp", lambda e, i0=i0, n_=n_, c0=c0: e.dma_start(
                            out=cbs_v[:, c0:c0 + n_ * 128], in_=tailT[3:51, i0:i0 + n_, :].rearrange("p a b -> p (a b)")),
                            reads=[tailTt], tok=tailTt)
                    yield
                yield "A_done"
                hTg = hT[:, g, :]
                hTgt = hTts[g]
                hsl = slice(g * 8, (g + 1) * 8)
                if main:
                    hb, hbt = hbf[0]
                    S.op("act", lambda e, hb=hb: e.copy(out=hb, in_=hTg), reads=[hTgt], writes=[hbt])
                nchunk = 9 if main else 8
                st = {}

                def load_h0(b):
                    hn, hnt = h0n[b % 4]
                    S.dma("sp", lambda e, hn=hn, b=b: e.dma_start(out=hn, in_=ssm_in[b, :, g, :, :]), writes=[hnt])

                def front(c):
                    gc = c + (NCH_PREV if main else 0)
                    k2 = cnt[0] % 2
                    cnt[0] += 1
                    sample = main and c == 8
                    d = st[c] = {"k2": k2}
                    pb, pbt = bank()
                    pbb = pb.bitcast(BF16).rearrange("p (a b) -> p a b", a=8)
                    for i in range(5):
                        tr(pbb[:, i, :], pbt, X(i, c), identb, [rawt, identbt])
                    xd, xdt_ = xdtd[k2]
                    bm_, bmt_ = btm[k2]
                    xs_tm = pbb[:, 0:4, :].rearrange("p a (h q) -> p (a h) q", h=2)
                    if main:
                        xz, xzt = xdtz[k2]
                        for hh in range(2):
                            S.op("dve", lambda e, xz=xz, hh=hh, xs_tm=xs_tm, c=c: e.tensor_tensor(
                                out=xz[:, hh::2, hh * 64:(hh + 1) * 64], in0=xs_tm[:, hh::2, :],
                                in1=dt9[:, c, g * 8 + hh:(g + 1) * 8:2].rearrange("p (h o) -> p h o", o=1).to_broadcast([128, 4, 64]),
                                op=ALU.mult), reads=[pbt, dt9t], writes=[xzt])
                    S.op("dve", lambda e, xd=xd, xs_tm=xs_tm, gc=gc: e.tensor_tensor(
                        out=xd.rearrange("p (h q) -> p h q", h=8), in0=xs_tm,
                        in1=wde_t[:, gc, hsl].rearrange("p (h o) -> p h o", o=1).to_broadcast([128, 8, 64]), op=ALU.mult),
                        reads=[pbt, wde_tt], writes=[xdt_])
                    S.op("act", lambda e, bm_=bm_, pbb=pbb: e.copy(out=bm_, in_=pbb[:, 4, :]), reads=[pbt], writes=[bmt_])
                    if not main:
                        return
                    pc, pct = bank()
                    mm(pc[:, 0:128], pct, X(4, c), X(5, c), [rawt])
                    cb_, cbt_ = cbT[k2]
                    S.op("act", lambda e, cb_=cb_, pc=pc: e.copy(out=cb_, in_=pc[:, 0:128]), reads=[pct], writes=[cbt_])
                    ab, abt = acb[k2]
                    if c < 2 and g == 0:
                        S.dma("sp", lambda e, ab=ab, c=c: e.dma_start(out=ab, in_=acum_scr[c, hsl, :].partition_broadcast(128)),
                              reads=[scrt], writes=[abt])
                    ea, eat = ea2[k2]
                    S.op("act", lambda e, ab=ab, ea=ea: e.activation(out=ea[0:64], in_=ab[0:64, 0::2, :], func=AF.Exp),
                         reads=[abt], writes=[eat])
                    S.op("act", lambda e, ab=ab, ea=ea: e.activation(out=ea[64:128], in_=ab[64:128, 1::2, :], func=AF.Exp),
                         reads=[abt], writes=[eat])
                    negm = P("negm_s") if sample else P("negm_p")
                    S.op("pool", lambda e, ab=ab, negm=negm: e.tensor_tensor(
                        out=ab, in0=ab, in1=negm.rearrange("p (o q) -> p o q", o=1).to_broadcast([128, 8, 128]), op=ALU.add),
                        reads=[abt, ppt], writes=[abt])
                    yield
                    S.op("dve", lambda e, ab=ab, c=c: e.tensor_tensor(
                        out=ab, in0=ab, in1=ac9[:, c, hsl].rearrange("p (h o) -> p h o", o=1).to_broadcast([128, 8, 128]),
                        op=ALU.subtract), reads=[abt, ac9t], writes=[abt])
                    e_, et_ = ee[k2]
                    S.op("act", lambda e, ab=ab, e_=e_: e.activation(out=e_, in_=ab, func=AF.Exp), reads=[abt], writes=[et_])
                    pg, pc = (g, c + 2) if c + 2 < nchunk else (g + 1, c + 2 - nchunk)
                    if pg < 8:
                        S.dma("sp", lambda e, ab=ab, pg=pg, pc=pc: e.dma_start(
                            out=ab, in_=acum_scr[pc, pg * 8:(pg + 1) * 8, :].partition_broadcast(128)), reads=[scrt], writes=[abt])
                    m_, mt_ = MT[k2]
                    S.op("pool", lambda e, m_=m_, e_=e_, cb_=cb_: e.tensor_tensor(
                        out=m_, in0=e_, in1=cb_.rearrange("p (o q) -> p o q", o=1).to_broadcast([128, 8, 128]), op=ALU.mult),
                        reads=[et_, cbt_], writes=[mt_])
                    if sample:
                        for b in range(4):
                            load_h0(b)

                def back(c):
                    gc = c + (NCH_PREV if main else 0)
                    k2 = st[c]["k2"]
                    sample = main and c == 8
                    xd, xdt_ = xdtd[k2]
                    bm_, bmt_ = btm[k2]
                    if main:
                        xz, xzt = xdtz[k2]
                        m_, mt_ = MT[k2]
                        ea, eat = ea2[k2]
                        pd, pdt = bank(4 + k2)
                        for j in range(4):
                            for hh in range(2):
                                mm(pd[:, j * 128:(j + 1) * 128], pdt, xz[:, 2 * j + hh, :], m_[:, 2 * j + hh, :], [xzt, mt_],
                                   start=(hh == 0), stop=(hh == 1))
                        po, pot = bank(6 + k2)
                        if not sample:
                            hb, hbt = hbf[c % 2]
                            for j in range(4):
                                mm(po[:, j * 128:(j + 1) * 128], pot, hb[:, j * 128:(j + 1) * 128], X(5, c), [hbt, rawt])
                        else:
                            S.op("dve", lambda e, bm_=bm_: e.tensor_tensor(
                                out=Bm, in0=bm_.rearrange("p (o n) -> p o n", o=1).to_broadcast([128, 16, 128]),
                                in1=P("seqmask").rearrange("p (b o) -> p b o", o=1).to_broadcast([128, 16, 128]), op=ALU.mult),
                                reads=[bmt_, ppt], writes=[Bmt])
                            S.op("dve", lambda e: e.tensor_copy(
                                out=dAx, in_=dA16[:, hsl].rearrange("p (h o) -> p h o", o=1).to_broadcast([128, 8, 64])),
                                reads=[dA16t], writes=[dAxt])
                            pq, pqt = bank()
                            dAxf = dAx.rearrange("p h q -> p (h q)")
                            for j in range(4):
                                mm(pq[:, j * 16:(j + 1) * 16], pqt, dAxf[:, j * 128:(j + 1) * 128], P("seqmask"), [dAxt, ppt])
                            S.op("act", lambda e, pq=pq: e.activation(out=cdn.rearrange("p a b -> p (a b)"), in_=pq[:, 0:64],
                                                                      func=AF.Exp), reads=[pqt], writes=[cdnt])
                            for b in range(16):
                                hn, hnt = h0n[b % 4]
                                h0, h0t = h0T[b % 2]
                                ptr, ptrt = bank()
                                for j in range(4):
                                    tr(ptr[:, j * 128:(j + 1) * 128], ptrt, hn[:, j, :], ident, [hnt, ppt])
                                S.op("act", lambda e, h0=h0, ptr=ptr: e.copy(out=h0, in_=ptr), reads=[ptrt], writes=[h0t])
                                for j in range(4):
                                    mm(po[:, j * 128 + 8 * b:j * 128 + 8 * b + 8], pot, h0[:, j * 128:(j + 1) * 128],
                                       raw[:, 5, 3 + c * 128 + 8 * b:3 + c * 128 + 8 * b + 8], [h0t, rawt])
                                pn, pnt = bank()
                                for j in range(4):
                                    mm(pn[:, j * 128:(j + 1) * 128], pnt, xd[:, j * 128:(j + 1) * 128], Bm[:, b, :], [xdt_, Bmt])
                                S.op("dve", lambda e, hn=hn, b=b: e.tensor_tensor(
                                    out=hn, in0=hn, in1=cdn[:, :, b:b + 1].to_broadcast([128, 4, 128]), op=ALU.mult),
                                    reads=[hnt, cdnt], writes=[hnt])
                                S.op("dve", lambda e, hn=hn, pn=pn: e.tensor_tensor(
                                    out=hn, in0=hn, in1=pn.rearrange("p (a b) -> p a b", a=4), op=ALU.add),
                                    reads=[hnt, pnt], writes=[hnt])
                                S.dma("sp", lambda e, hn=hn, b=b: e.dma_start(out=sss_v[b, :, g, :, :], in_=hn), reads=[hnt],
                                      tok=hnt)
                                if b + 4 < 16:
                                    load_h0(b + 4)
                        t_, tt_ = t1[k2]
                        S.op("dve", lambda e, t_=t_, po=po, ea=ea: e.tensor_tensor(
                            out=t_, in0=po.rearrange("p (a b) -> p a b", a=4), in1=ea, op=ALU.mult),
                            reads=[pot, eat], writes=[tt_])
                        S.op("dve", lambda e, t_=t_, pd=pd: e.tensor_tensor(
                            out=t_, in0=t_, in1=pd.rearrange("p (a b) -> p a b", a=4), op=ALU.add),
                            reads=[tt_, pdt], writes=[tt_])
                        yc, yct = ych[k2]
                        for j in range(4):
                            S.op("dve", lambda e, t_=t_, j=j, c=c, yc=yc: e.scalar_tensor_tensor(
                                out=yc[:, j, :], in0=X(j, c), scalar=P("drow", g * 4 + j, g * 4 + j + 1),
                                in1=t_[:, j, :], op0=ALU.mult, op1=ALU.add), reads=[rawt, ppt, tt_], writes=[yct])
                        S.dma("sp", lambda e, yc=yc, c=c: e.dma_start(
                            out=un_scr[g * 4:(g + 1) * 4, :, c * 128:(c + 1) * 128].rearrange("t p n -> p t n"), in_=yc),
                            reads=[yct], writes=[unscrt], tok=yct)
                    if not sample:
                        pst, pstt = bank()
                        mm(pst, pstt, bm_, xd, [bmt_, xdt_])
                        S.op("pool", lambda e, gc=gc: e.tensor_tensor(
                            out=hTg.rearrange("p (h q) -> p h q", h=8), in0=hTg.rearrange("p (h q) -> p h q", h=8),
                            in1=cdb_t[:, gc, hsl].rearrange("p (h o) -> p h o", o=1).to_broadcast([128, 8, 64]), op=ALU.mult),
                            reads=[hTgt, cdb_tt], writes=[hTgt])
                        S.op("dve", lambda e, pst=pst: e.tensor_tensor(out=hTg, in0=hTg, in1=pst, op=ALU.add),
                             reads=[hTgt, pstt], writes=[hTgt])
                        if main and c < 7:
                            hb, hbt = hbf[(c + 1) % 2]
                            S.op("act", lambda e, hb=hb: e.copy(out=hb, in_=hTg), reads=[hTgt], writes=[hbt])
                        if main and c == 7:
                            pf, pft = bank()
                            for j in range(4):
                                tr(pf[:, j * 128:(j + 1) * 128], pft, hTg[:, j * 128:(j + 1) * 128], ident, [hTgt, ppt])
                            S.op("act", lambda e, pf=pf: e.copy(out=hnat.rearrange("p a b -> p (a b)"), in_=pf),
                                 reads=[pft], writes=[hnatt])
                            S.dma("sp", lambda e: e.dma_start(out=ssp_v[:, g, :, :], in_=hnat), reads=[hnatt], tok=hnatt)

                def run_rest(it):
                    for _ in it:
                        pass
                f0 = front(0)
                run_rest(f0)
                yield "F0_done"
                for c in range(nchunk):
                    fn = front(c + 1) if c + 1 < nchunk else iter(())
                    next(fn, None)
                    pump()
                    run_rest(fn)
                    if c == nchunk - 1:
                        pump(front0=True)
                    back(c)
                if not main:
                    S.op("dve", lambda e: e.tensor_scalar(out=hTg, in0=hTg, scalar1=P("flag"), scalar2=None, op0=ALU.mult),
                         reads=[hTgt, ppt], writes=[hTgt])

            def drive(mode):
                its = {}

                def finish_a(nx):
                    while not nx["done"]:
                        if next(nx["it"]) == "A_done":
                            nx["done"] = True

                def make_pump(g):
                    def pump(front0=False):
                        nx = its.get(g + 1)
                        if nx is None:
                            return
                        if front0:
                            finish_a(nx)
                            if not nx["f0"]:
                                while next(nx["it"]) != "F0_done":
                                    pass
                                nx["f0"] = True
                            return
                        if nx["done"]:
                            return
                        for _ in range(3):
                            if next(nx["it"]) == "A_done":
                                nx["done"] = True
                                return
                    return pump
                for g in range(8):
                    its[g] = {"it": ssd_group(g, mode, make_pump(g)), "done": False, "f0": False}
                for g in range(8):
                    cur = its[g]
                    finish_a(cur)
                    if not cur["f0"]:
                        while next(cur["it"]) != "F0_done":
                            pass
                        cur["f0"] = True
                    for _ in cur["it"]:
                        pass

            drive("prev")
            dump("hT", hT.rearrange("p a b -> p (a b)"), hTts[7])
            if upto >= 4:
                S.barrier()
                A.release_top()
                alloc_s2()
                drive("main")
        S.barrier()
        A.release(mP0)

        BLK = ((0, 512), (512, 512), (1024, 128))

        def half_slab(src_cols_ap, kt=16):
            b_, bt_ = ring[ring_i[0] % NSLOT]
            ring_i[0] += 1
            v = b_.rearrange("p a b -> p (a b)")[:, 0:kt * 256].rearrange("p (a b) -> p a b", a=kt)
            S.dma("pool", lambda e: e.dma_start(out=v, in_=src_cols_ap.rearrange("(kt p) f -> p kt f", p=128)), writes=[bt_])
            return v, bt_

        def acc_banks(fi):
            return [(banks[2 * fi][0][:, 0:512], banks[2 * fi][1]), (banks[2 * fi + 1][0][:, 0:512], banks[2 * fi + 1][1]),
                    (banks[4][0][:, fi * 128:(fi + 1) * 128], banks[4][1])]

        def pair_proj(w, wt, nkt, rhs_of, rhs_toks, kt0=0, first=True, last=True):
            for fi in range(2):
                acc = acc_banks(fi)
                for bi, (t0, tn) in enumerate(BLK):
                    for kt in range(nkt):
                        mm(acc[bi][0], acc[bi][1], w[:, kt, fi * 128:(fi + 1) * 128], rhs_of(kt0 + kt, t0, tn), [wt] + rhs_toks,
                           start=(first and kt == 0 and not (bi == 2 and fi == 1)), stop=(last and kt == nkt - 1), skip=(bi == 2))

        if upto >= 5:

            def gate_pair(fp, goff):
                wg, wgt = half_slab(w_in[:, goff + fp * 256:goff + (fp + 1) * 256])
                pair_proj(wg, wgt, 16, lambda kt, t0, tn: xnM[:, kt, t0:t0 + tn], [xnMt])
                for fi in range(2):
                    acc = acc_banks(fi)
                    for bi, (t0, tn) in enumerate(BLK):
                        S.op("act", lambda e, a=acc[bi][0], fi=fi, t0=t0, tn=tn: e.activation(
                            out=tg_[:, fi, t0:t0 + tn], in_=a, func=AF.Tanh, scale=0.5), reads=[acc[bi][1]], writes=[tgt_])

            def merge_pair(fp, first_term):
                for fi in range(2):
                    acc = acc_banks(fi)
                    f = fp * 2 + fi
                    for bi, (t0, tn) in enumerate(BLK):
                        if first_term:
                            S.op("dve", lambda e, a=acc[bi][0], fi=fi, f=f, t0=t0, tn=tn: e.scalar_tensor_tensor(
                                out=mixedT[:, f, t0:t0 + tn], in0=tg_[:, fi, t0:t0 + tn], scalar=1.0, in1=a,
                                op0=ALU.add, op1=ALU.mult), reads=[tgt_, acc[bi][1]], writes=[mixedTt])
                        else:
                            S.op("dve", lambda e, a=acc[bi][0], fi=fi, t0=t0, tn=tn: e.scalar_tensor_tensor(
                                out=mtmp[:, t0:t0 + tn], in0=tg_[:, fi, t0:t0 + tn], scalar=1.0, in1=a,
                                op0=ALU.add, op1=ALU.mult), reads=[tgt_, acc[bi][1]], writes=[mtmpt])
                            S.op("dve", lambda e, f=f, t0=t0, tn=tn: e.tensor_tensor(
                                out=mixedT[:, f, t0:t0 + tn], in0=mixedT[:, f, t0:t0 + tn], in1=mtmp[:, t0:t0 + tn], op=ALU.add),
                                reads=[mtmpt, mixedTt], writes=[mixedTt])

            unT, unTt = A.alloc("unT", [32, NMAIN], BF16, top=True)
            mG = A.mark()
            ug, ugt = A.alloc("ug", [4, NMAIN], F32)
            u2, u2t = A.alloc("u2", [4, NMAIN], BF16)
            gth, gtht = A.alloc("gth", [512], F32)
            gzz, gzzt = A.alloc("gzz", [512], F32)
            grs, grst = A.alloc("grs", [NMAIN], F32)
            onesb, onesbt = A.alloc("onesb5", [128], BF16)
            S.op("dve", lambda e: e.memset(onesb, 1.0), writes=[onesbt])
            for g in range(8):
                S.dma("sp", lambda e, g=g: e.dma_start(out=unT[:, g * 4:(g + 1) * 4, :],
                                                      in_=un_scr[g * 4:(g + 1) * 4].rearrange("t p n -> p t n")),
                      reads=[unscrt], writes=[unTt])
                wz, wzt = wslab(w_in[:, OFF_Z + g * 512:OFF_Z + (g + 1) * 512])
                for j in range(4):
                    for (t0, tn) in BLK:
                        pb, pbt = bank()
                        for kt in range(16):
                            mm(pb[:, 0:tn], pbt, wz[:, kt, j * 128:(j + 1) * 128], xnM[:, kt, t0:t0 + tn], [wzt, xnMt],
                               start=(kt == 0), stop=(kt == 15))
                        S.op("act", lambda e, pb=pb, tn=tn: e.activation(out=gth[:, 0:tn], in_=pb[:, 0:tn], func=AF.Tanh, scale=0.5),
                             reads=[pbt], writes=[gtht])
                        S.op("dve", lambda e, pb=pb, tn=tn: e.scalar_tensor_tensor(
                            out=gzz[:, 0:tn], in0=gth[:, 0:tn], scalar=1.0, in1=pb[:, 0:tn], op0=ALU.add, op1=ALU.mult),
                            reads=[gtht, pbt], writes=[gzzt])
                        S.op("dve", lambda e, g=g, j=j, t0=t0, tn=tn: e.tensor_tensor(
                            out=ug[:, j, t0:t0 + tn], in0=unT[:, g * 4 + j, t0:t0 + tn], in1=gzz[:, 0:tn], op=ALU.mult),
                            reads=[unTt, gzzt], writes=[ugt])
                    S.op("act", lambda e, j=j: e.activation(out=u2[:, j, :], in_=ug[:, j, :], func=AF.Square),
                         reads=[ugt], writes=[u2t])
                for (t0, tn) in BLK:
                    pb, pbt = bank()
                    for j in range(4):
                        mm(pb[:, 0:tn], pbt, onesb, u2[:, j, t0:t0 + tn], [onesbt, u2t], start=(j == 0), stop=(j == 3))
                    S.op("act", lambda e, pb=pb, t0=t0, tn=tn: e.activation(
                        out=grs[:, t0:t0 + tn], in_=pb[:, 0:tn], func=AF.Ln, scale=1.0 / 512, bias=float(4 * EPS)),
                        reads=[pbt], writes=[grst])
                S.op("act", lambda e: e.activation(out=grs, in_=grs, func=AF.Exp, scale=-0.5), reads=[grst], writes=[grst])
                for j in range(4):
                    S.op("dve", lambda e, g=g, j=j: e.scalar_tensor_tensor(
                        out=unT[:, g * 4 + j, :], in0=ug[:, j, :], scalar=P("gssm", g * 4 + j, g * 4 + j + 1), in1=grs,
                        op0=ALU.mult, op1=ALU.mult), reads=[ugt, ppt, grst], writes=[unTt])
            S.barrier()
            A.release(mG)
            mixedT, mixedTt = A.alloc("mixedT", [16, NMAIN], BF16)
            tg_, tgt_ = A.alloc("tg", [2, NMAIN], F32)
            mtmp, mtmpt = A.alloc("mtmp", [NMAIN], F32)
            mC = A.mark()
            dump("unT", unT.rearrange("p a b -> p (a b)"), unTt)
            for fp in range(8):
                gate_pair(fp, OFF_G + 2048)
                for rb in range(2):
                    wb, wbt = half_slab(w_out_b[rb * 2048:(rb + 1) * 2048, fp * 256:(fp + 1) * 256])
                    pair_proj(wb, wbt, 16, lambda kt, t0, tn: unT[:, kt, t0:t0 + tn], [unTt], kt0=rb * 16,
                              first=(rb == 0), last=(rb == 1))
                merge_pair(fp, True)
            dump("mixB", mixedT.rearrange("p a b -> p (a b)"), mixedTt)
            S.barrier()
            A.release(mC)
            A.release_top()

        if upto >= 6:
            uaT, uaTt = A.alloc("uaT", [16, NMAIN], BF16)
            CHW = 2 + 1024 + 160
            chb, chbt = A.alloc("chb", [CHW], F32)
            bsb, bsbt = A.alloc("bsb", [NMAIN], F32)
            cva, cvat = A.alloc("cva", [NMAIN], F32)
            tla, tlat = A.alloc("tla", [34], F32)
            capT, capTt = A.alloc("capT", [16, 128], F32)
            sca_in, sca_int = A.alloc("sca_in", [D], F32)
            scaT, scaTt = A.alloc("scaT", [16, 32], F32)
            S.dma("sp", lambda e: e.dma_start(out=sca_in[0:32, :], in_=sca), writes=[sca_int])
            for k4 in range(4):
                pb, pbt = bank(5)
                for k in range(4):
                    f = k4 * 4 + k
                    tr(pb[:, k * 32:(k + 1) * 32], pbt, sca_in[0:32, f * 128:(f + 1) * 128], ident[0:32, 0:32], [sca_int, ppt])
                S.op("act", lambda e, pb=pb, k4=k4: e.copy(out=scaT[:, k4 * 4:(k4 + 1) * 4, :].rearrange("p a b -> p (a b)"),
                                                           in_=pb[:, 0:128]), reads=[pbt], writes=[scaTt])
            wca = P("wca").rearrange("p (t k) -> p t k", k=3)
            for f in range(16):
                b_, bt_ = ring[ring_i[0] % NSLOT]
                ring_i[0] += 1
                w3 = b_.rearrange("p a b -> p (a b)")[:, 0:16 * 384].rearrange("p (a b) -> p a b", a=16)
                for q, off in enumerate((OFF_BA, OFF_CA, OFF_HA)):
                    S.dma("pool", lambda e, q=q, off=off, f=f, w3=w3: e.dma_start(
                        out=w3[:, :, q * 128:(q + 1) * 128],
                        in_=w_in[:, off + f * 128:off + (f + 1) * 128].rearrange("(kt p) f -> p kt f", p=128)), writes=[bt_])
                if True:
                    ph, pht = bank(5)
                    for q in (1, 2):
                        for kt in range(16):
                            mm(ph[:, (q - 1) * 2:(q - 1) * 2 + 2], pht, w3[:, kt, q * 128:(q + 1) * 128],
                               xnP2[:, kt, :], [bt_, xnP2t], start=(kt == 0), stop=(kt == 15))
                    S.op("act", lambda e, ph=ph: e.copy(out=chb[:, 0:2], in_=ph[:, 0:2]), reads=[pht], writes=[chbt])
                    S.op("dve", lambda e, ph=ph: e.tensor_tensor(out=chb[:, 0:2], in0=chb[:, 0:2], in1=ph[:, 2:4], op=ALU.mult),
                         reads=[pht, chbt], writes=[chbt])
                    chs = chb[:, 1026:1186].rearrange("p (b k) -> p b k", k=10)
                    S.op("dve", lambda e, chs=chs, f=f: e.tensor_copy(
                        out=chs[:, :, 0:2], in_=scaT[:, f, :].rearrange("p (b k) -> p b k", k=2)), reads=[scaTt], writes=[chbt])
                    for q in (1, 2, 0):
                        for bi, (t0, tn) in enumerate(BLK):
                            pb, pbt = bank()
                            for kt in range(16):
                                mm(pb[:, 0:tn], pbt, w3[:, kt, q * 128:(q + 1) * 128], xnM[:, kt, t0:t0 + tn],
                                   [bt_, xnMt], start=(kt == 0), stop=(kt == 15))
                            if bi < 2:
                                dst = chb[:, 2 + t0:2 + t0 + 512]
                                src_ = pb[:, 0:512]
                            else:
                                dst = chs[:, :, 2:10]
                                src_ = pb[:, 0:128].rearrange("p (b k) -> p b k", k=8)
                            if q == 1:
                                S.op("act", lambda e, dst=dst, src_=src_: e.copy(out=dst, in_=src_), reads=[pbt], writes=[chbt])
                            elif q == 2:
                                S.op("dve", lambda e, dst=dst, src_=src_: e.tensor_tensor(out=dst, in0=dst, in1=src_, op=ALU.mult),
                                     reads=[pbt, chbt], writes=[chbt])
                            else:
                                S.op("act", lambda e, pb=pb, t0=t0, tn=tn: e.copy(out=bsb[:, t0:t0 + tn], in_=pb[:, 0:tn]),
                                     reads=[pbt], writes=[bsbt])
                    S.op("dve", lambda e: e.tensor_copy(out=tla[:, 0:2], in_=chb[:, 1024:1026]), reads=[chbt], writes=[tlat])
                    S.op("dve", lambda e, chs=chs: e.tensor_copy(out=tla[:, 2:34].rearrange("p (b k) -> p b k", k=2),
                                                                  in_=chs[:, :, 8:10]), reads=[chbt], writes=[tlat])
                    pt_, ptt_ = bank(5)
                    tr(pt_[0:34, 0:128], ptt_, tla, ident, [tlat, ppt])
                    S.op("act", lambda e, pt_=pt_, f=f: e.copy(out=capT[0:34, f, :], in_=pt_[0:34, 0:128]), reads=[ptt_], writes=[capTt])
                    S.op("act", lambda e, f=f: e.activation(out=cva[:, 0:1024], in_=chb[:, 2:1026], func=AF.Identity,
                                                             scale=wca[:, f, 2:3]), reads=[chbt, ppt], writes=[cvat])
                    cvs_ = cva[:, 1024:1152].rearrange("p (b k) -> p b k", k=8)
                    S.op("act", lambda e, f=f, chs=chs, cvs_=cvs_: e.activation(out=cvs_, in_=chs[:, :, 2:10], func=AF.Identity,
                                                                               scale=wca[:, f, 2:3]), reads=[chbt, ppt], writes=[cvat])
                    for k in range(2):
                        S.op("dve", lambda e, f=f, k=k: e.scalar_tensor_tensor(
                            out=cva[:, 0:1024], in0=chb[:, k:k + 1024], scalar=wca[:, f, k:k + 1], in1=cva[:, 0:1024],
                            op0=ALU.mult, op1=ALU.add), reads=[chbt, ppt, cvat], writes=[cvat])
                        S.op("dve", lambda e, f=f, k=k, chs=chs, cvs_=cvs_: e.scalar_tensor_tensor(
                            out=cvs_, in0=chs[:, :, k:k + 8], scalar=wca[:, f, k:k + 1], in1=cvs_,
                            op0=ALU.mult, op1=ALU.add), reads=[chbt, ppt, cvat], writes=[cvat])
                    S.op("dve", lambda e, f=f: e.tensor_tensor(out=uaT[:, f, :], in0=bsb, in1=cva, op=ALU.mult),
                         reads=[bsbt, cvat], writes=[uaTt])
            S.dma("sp", lambda e: e.dma_start(out=cap_o, in_=capT[0:2].rearrange("p a b -> p (a b)")), reads=[capTt], tok=capTt)
            S.dma("sp", lambda e: e.dma_start(out=cas_o.rearrange("b k f -> (b k) f"), in_=capT[2:34].rearrange("p a b -> p (a b)")),
                  reads=[capTt], tok=capTt)
            dump("uaT", uaT.rearrange("p a b -> p (a b)"), uaTt)
            for fp in range(8):
                gate_pair(fp, OFF_G)
                wa, wat = half_slab(w_out_a[:, fp * 256:(fp + 1) * 256])
                pair_proj(wa, wat, 16, lambda kt, t0, tn: uaT[:, kt, t0:t0 + tn], [uaTt])
                merge_pair(fp, False)
            dump("mixA", mixedT.rearrange("p a b -> p (a b)"), mixedTt)
            S.barrier()
            A.release(mC)

        if upto >= 7:
            SC = 128 ** -0.5
            KT, KTt = A.alloc("KT", [4, 256], BF16)
            Vb, Vbt = A.alloc("Vb", [2, 512], BF16)
            qT, qTt = A.alloc("qT", [4, NMAIN], BF16)
            oT, oTt = A.alloc("oT", [4, NMAIN], BF16)
            mD = A.mark()
            memnT, memnTt = A.alloc("memnT", [16, 256], BF16)
            kvo = [A.alloc(f"kvo{i}", [512], F32) for i in range(2)]
            mx_in = [A.alloc(f"m_xin{i}", [D], F32) for i in range(2)]
            mx_s, mx_st = A.alloc("m_xs", [D], BF16)
            mj, mjt = A.alloc("m_junk", [D], BF16)
            msq, msqt = A.alloc("m_ssq", [2], F32)
            mr, mrt = A.alloc("m_r", [2], F32)
            for c in range(2):
                xi, xit = mx_in[c]
                S.dma("sp", lambda e, xi=xi, c=c: e.dma_start(out=xi, in_=memp[c * 128:(c + 1) * 128, :]), writes=[xit])
                S.op("act", lambda e, xi=xi, c=c: e.activation(out=mj, in_=xi, func=AF.Square, accum_out=msq[:, c:c + 1]),
                     reads=[xit], writes=[mjt, msqt])
            rstd_of(mr, mrt, msq, msqt, D, EPS)
            for c in range(2):
                xi, xit = mx_in[c]
                S.op("dve", lambda e, xi=xi, c=c: e.tensor_scalar(out=mx_s, in0=xi, scalar1=mr[:, c:c + 1], scalar2=None, op0=ALU.mult),
                     reads=[xit, mrt], writes=[mx_st])
                for half in range(2):
                    pb, pbt = bank()
                    pbb = pb.bitcast(BF16).rearrange("p (a b) -> p a b", a=8)
                    for k in range(8):
                        kt = half * 8 + k
                        tr(pbb[:, k, :], pbt, mx_s[:, kt * 128:(kt + 1) * 128], identb, [mx_st, identbt])
                    for k in range(8):
                        kt = half * 8 + k
                        S.op("dve" if k % 2 else "act", (lambda e, pbb=pbb, k=k, kt=kt, c=c: e.tensor_scalar(
                            out=memnT[:, kt, c * 128:(c + 1) * 128], in0=pbb[:, k, :], scalar1=P("gmem", kt, kt + 1), scalar2=None,
                            op0=ALU.mult)) if k % 2 else (lambda e, pbb=pbb, k=k, kt=kt, c=c: e.activation(
                                out=memnT[:, kt, c * 128:(c + 1) * 128], in_=pbb[:, k, :], func=AF.Copy, scale=P("gmem", kt, kt + 1))),
                            reads=[pbt, ppt], writes=[memnTt])
            wk, wkt = wslab(w_mem_kv[:, 0:512])
            for h in range(4):
                pb, pbt = bank()
                for kt in range(16):
                    mm(pb[:, 0:256], pbt, wk[:, kt, h * 128:(h + 1) * 128], memnT[:, kt, :], [wkt, memnTt], start=(kt == 0), stop=(kt == 15))
                S.op("act", lambda e, pb=pb, h=h: e.copy(out=KT[:, h, :], in_=pb[:, 0:256]), reads=[pbt], writes=[KTt])
            for mt in range(2):
                pb, pbt = bank()
                for kt in range(16):
                    mm(pb, pbt, memnT[:, kt, mt * 128:(mt + 1) * 128], wk[:, kt, :], [wkt, memnTt], start=(kt == 0), stop=(kt == 15))
                ko, kot = kvo[mt]
                S.op("act", lambda e, pb=pb, ko=ko: e.copy(out=ko, in_=pb), reads=[pbt], writes=[kot])
                S.dma("sp", lambda e, ko=ko, mt=mt: e.dma_start(out=memk_o[mt * 128:(mt + 1) * 128, :], in_=ko), reads=[kot], tok=kot)
            wv, wvt = wslab(w_mem_kv[:, 512:1024])
            for mt in range(2):
                pb, pbt = bank()
                for kt in range(16):
                    mm(pb, pbt, memnT[:, kt, mt * 128:(mt + 1) * 128], wv[:, kt, :], [wvt, memnTt], start=(kt == 0), stop=(kt == 15))
                ko, kot = kvo[mt]
                S.op("act", lambda e, pb=pb, ko=ko: e.copy(out=ko, in_=pb), reads=[pbt], writes=[kot])
                S.op("dve", lambda e, pb=pb, mt=mt: e.tensor_copy(out=Vb[:, mt, :], in_=pb), reads=[pbt], writes=[Vbt])
                S.dma("sp", lambda e, ko=ko, mt=mt: e.dma_start(out=memv_o[mt * 128:(mt + 1) * 128, :], in_=ko), reads=[kot], tok=kot)
            wq, wqt = wslab(w_in[:, OFF_Q:OFF_Q + 512])
            for h in range(4):
                for (t0, tn) in BLK:
                    pb, pbt = bank()
                    for kt in range(16):
                        mm(pb[:, 0:tn], pbt, wq[:, kt, h * 128:(h + 1) * 128], xnM[:, kt, t0:t0 + tn], [wqt, xnMt],
                           start=(kt == 0), stop=(kt == 15))
                    S.op("act", lambda e, pb=pb, h=h, t0=t0, tn=tn: e.copy(out=qT[:, h, t0:t0 + tn], in_=pb[:, 0:tn]),
                         reads=[pbt], writes=[qTt])
            S.barrier()
            A.release(mD)
            mxs, mxst = A.alloc("a_mx", [4], F32)
            nb_, nbt_ = A.alloc("a_nb", [4], F32)
            ssum, ssumt = A.alloc("a_ss", [4], F32)
            rsum, rsumt = A.alloc("a_rs", [4], F32)
            ex_ = [A.alloc(f"a_e{i}", [256], F32) for i in range(2)]
            pn = [A.alloc(f"a_pn{i}", [4, 256], BF16) for i in range(2)]
            pT = [A.alloc(f"a_pT{i}", [8, 128], BF16) for i in range(2)]

            def softmax_rows(ps_list, k2):
                p_, pt_ = pn[k2]
                for h, (ps, pst_) in enumerate(ps_list):
                    S.op("dve", lambda e, ps=ps, h=h: e.reduce_max(out=mxs[:, h:h + 1], in_=ps, axis=AX.X), reads=[pst_], writes=[mxst])
                S.op("dve", lambda e: e.tensor_scalar(out=nb_, in0=mxs, scalar1=-SC, scalar2=None, op0=ALU.mult), reads=[mxst], writes=[nbt_])
                for h, (ps, pst_) in enumerate(ps_list):
                    ex, ext = ex_[h % 2]
                    S.op("act", lambda e, ps=ps, h=h, ex=ex: e.activation(out=ex, in_=ps, func=AF.Exp, scale=SC, bias=nb_[:, h:h + 1],
                                                                         accum_out=ssum[:, h:h + 1]), reads=[pst_, nbt_], writes=[ext, ssumt])
                    S.op("dve", lambda e, h=h: e.reciprocal(out=rsum[:, h:h + 1], in_=ssum[:, h:h + 1]), reads=[ssumt], writes=[rsumt])
                    S.op("dve", lambda e, h=h, ex=ex, p_=p_: e.tensor_scalar(out=p_[:, h, :], in0=ex, scalar1=rsum[:, h:h + 1], scalar2=None,
                                                                            op0=ALU.mult), reads=[ext, rsumt], writes=[pt_])
                return p_, pt_

            for c in range(8):
                k2 = c % 2
                ps_list = []
                for h2 in range(2):
                    pb, pbt = bank(4 + h2)
                    for hh in range(2):
                        h = h2 * 2 + hh
                        mm(pb[:, hh * 256:(hh + 1) * 256], pbt, qT[:, h, c * 128:(c + 1) * 128], KT[:, h, :], [qTt, KTt])
                        ps_list.append((pb[:, hh * 256:(hh + 1) * 256], pbt))
                p_, pt_ = softmax_rows(ps_list, k2)
                ptr_, ptrt_ = bank()
                ptb = ptr_.bitcast(BF16).rearrange("p (a b) -> p a b", a=8)
                for h in range(4):
                    for mt in range(2):
                        tr(ptb[:, h * 2 + mt, :], ptrt_, p_[:, h, mt * 128:(mt + 1) * 128], identb, [pt_, identbt])
                pt2, pt2t = pT[k2]
                S.op("act", lambda e, pt2=pt2, ptb=ptb: e.copy(out=pt2, in_=ptb), reads=[ptrt_], writes=[pt2t])
                po_, pot_ = bank()
                for h in range(4):
                    for mt in range(2):
                        mm(po_[:, h * 128:(h + 1) * 128], pot_, Vb[:, mt, h * 128:(h + 1) * 128], pt2[:, h * 2 + mt, :], [Vbt, pt2t],
                           start=(mt == 0), stop=(mt == 1))
                S.op("dve", lambda e, po_=po_, c=c: e.tensor_copy(out=oT[:, :, c * 128:(c + 1) * 128],
                                                                  in_=po_.rearrange("p (h l) -> p h l", h=4)), reads=[pot_], writes=[oTt])
            kcb, kcbt = A.alloc("kcb", [16, 2, 128], BF16)
            vcb, vcbt = A.alloc("vcb", [16, 2, 128], BF16)
            KsT, KsTt = A.alloc("KsT", [16, 256], BF16)
            qTm, qTmt = A.alloc("qTm", [16, 128], BF16)
            S.op("dve", lambda e: e.memset(qTm, 0.0), writes=[qTmt])
            kc_v = kc.rearrange("b (mt p) (hd d) -> p b mt hd d", p=128, d=128)
            vc_v = vc.rearrange("b (mt p) (hd d) -> p b mt hd d", p=128, d=128)
            for h in range(4):
                for b in range(16):
                    S.dma("pool", lambda e, h=h, b=b: e.dma_start(out=kcb[:, b, :, :], in_=kc_v[:, b, :, h, :]), writes=[kcbt])
                    S.dma("pool", lambda e, h=h, b=b: e.dma_start(out=vcb[:, b, :, :], in_=vc_v[:, b, :, h, :]), writes=[vcbt])
                for b4 in range(4):
                    ptr_, ptrt_ = bank()
                    ptb = ptr_.bitcast(BF16).rearrange("p (a b) -> p a b", a=8)
                    for bb in range(4):
                        for mt in range(2):
                            tr(ptb[:, bb * 2 + mt, :], ptrt_, kcb[:, b4 * 4 + bb, mt, :], identb, [kcbt, identbt])
                    S.op("act" if b4 % 2 else "dve", (lambda e, ptb=ptb, b4=b4: e.copy(
                        out=KsT[:, b4 * 4:(b4 + 1) * 4, :].rearrange("p a b -> p (a b)"), in_=ptb.rearrange("p a b -> p (a b)")))
                        if b4 % 2 else (lambda e, ptb=ptb, b4=b4: e.tensor_copy(
                            out=KsT[:, b4 * 4:(b4 + 1) * 4, :].rearrange("p a b -> p (a b)"), in_=ptb.rearrange("p a b -> p (a b)"))),
                        reads=[ptrt_], writes=[KsTt])
                for b in range(16):
                    S.op("dve", lambda e, h=h, b=b: e.tensor_copy(out=qTm[:, b, 8 * b:8 * b + 8],
                                                                   in_=qT[:, h, 1024 + 8 * b:1024 + 8 * b + 8]), reads=[qTt], writes=[qTmt])
                pb, pbt = bank(4)
                for b in range(16):
                    mm(pb[:, 0:256], pbt, qTm[:, b, :], KsT[:, b, :], [qTmt, KsTt], start=(b == 0), stop=(b == 15))
                p_, pt_ = softmax_rows([(pb[:, 0:256], pbt)], h % 2)
                ptr_, ptrt_ = bank()
                ptb = ptr_.bitcast(BF16).rearrange("p (a b) -> p a b", a=8)
                for mt in range(2):
                    tr(ptb[:, mt, :], ptrt_, p_[:, 0, mt * 128:(mt + 1) * 128], identb, [pt_, identbt])
                pt2, pt2t = pT[h % 2]
                S.op("act", lambda e, pt2=pt2, ptb=ptb: e.copy(out=pt2[:, 0:2, :], in_=ptb[:, 0:2, :]), reads=[ptrt_], writes=[pt2t])
                po_, pot_ = bank(5)
                for b in range(16):
                    for mt in range(2):
                        mm(po_[:, 8 * b:8 * b + 8], pot_, vcb[:, b, mt, :], pt2[:, mt, 8 * b:8 * b + 8], [vcbt, pt2t],
                           start=(mt == 0), stop=(mt == 1))
                S.op("dve", lambda e, po_=po_, h=h: e.tensor_copy(out=oT[:, h, 1024:1152], in_=po_[:, 0:128]), reads=[pot_], writes=[oTt])
            dump("oT", oT.rearrange("p a b -> p (a b)"), oTt)
            wxo, bt_ = A.alloc("wxo", [4, D], BF16)
            S.dma("pool", lambda e: e.dma_start(out=wxo, in_=w_out_x.rearrange("(kt p) f -> p kt f", p=128)), writes=[bt_])
            for fp in range(8):
                gate_pair(fp, OFF_G + 4096)
                pair_proj(wxo[:, :, fp * 256:(fp + 1) * 256], bt_, 4, lambda kt, t0, tn: oT[:, kt, t0:t0 + tn], [oTt])
                merge_pair(fp, False)
            dump("mixX", mixedT.rearrange("p a b -> p (a b)"), mixedTt)
            S.op("dve", lambda e: e.tensor_copy(out=xnM, in_=mixedT), reads=[mixedTt], writes=[xnMt])
            S.barrier()
            A.release(mP0)

        if upto >= 8:
            mixT, mixTt = xnM, xnMt
            gpo, gpot = A.alloc("gpost_b", [D], F32)
            gmo, gmot = A.alloc("gmpost_b", [D], F32)
            S.dma("sp", lambda e: e.dma_start(out=gpo, in_=g_post.partition_broadcast(128).rearrange("p a d -> p (a d)")), writes=[gpot])
            S.dma("sp", lambda e: e.dma_start(out=gmo, in_=g_mpost.partition_broadcast(128).rearrange("p a d -> p (a d)")), writes=[gmot])
            x2 = [A.alloc(f"x2_{i}", [D], F32) for i in range(3)]
            ffa, ffat = A.alloc("ffo", [3, D], F32)
            ffo = [(ffa[:, i, :], ffat) for i in range(3)]
            xl = [A.alloc("xl0", [D], F32)] * 2
            hff, hfft = A.alloc("hff", [64, 384], BF16)
            rtmp = [A.alloc(f"rtmp{i}", [384], F32) for i in range(2)]
            hnb, hnbt = A.alloc("hnb", [D], BF16)
            sq8, sq8t = A.alloc("sq8", [4], F32)
            r8, r8t = A.alloc("r8", [4], F32)
            junk8, junk8t = hnb, hnbt
            hnT = ffa.rearrange("p a b -> p (a b)").bitcast(BF16)[:, 0:16 * 384].rearrange("p (a b) -> p a b", a=16)
            hnTt = ffat
            ctoks = [Tok(f"wc{i}") for i in range(36)]

            def p8slab(idx, src_ap, tg3):
                if tg3 == 0:
                    v, t = wslab(src_ap)
                    S.dma("sp", lambda e: e.dma_start(out=wcache[idx], in_=v.rearrange("p a b -> p (a b)")), reads=[t],
                          writes=[ctoks[idx]], tok=t)
                    return v, t
                b_, bt_ = ring[ring_i[0] % NSLOT]
                ring_i[0] += 1
                S.dma("sp", lambda e: e.dma_start(out=b_.rearrange("p a b -> p (a b)"), in_=wcache[idx]), reads=[ctoks[idx]],
                      writes=[bt_])
                return b_, bt_

            for tg3 in range(3):
                ch0 = tg3 * 3
                for cb4 in range(4):
                    wo, wot = p8slab(cb4, w_o[:, cb4 * 512:(cb4 + 1) * 512], tg3)
                    for ci in range(3):
                        c = ch0 + ci
                        pb, pbt = bank()
                        for kt in range(16):
                            mm(pb, pbt, mixT[:, kt, c * 128:(c + 1) * 128], wo[:, kt, :], [mixTt, wot], start=(kt == 0), stop=(kt == 15))
                        xx, xxt = x2[ci]
                        S.op("act", lambda e, pb=pb, xx=xx, cb4=cb4: e.copy(out=xx[:, cb4 * 512:(cb4 + 1) * 512], in_=pb),
                             reads=[pbt], writes=[xxt])
                for ci in range(3):
                    c = ch0 + ci
                    xx, xxt = x2[ci]
                    xl_, xlt_ = xl[ci % 2]
                    S.dma("sp", lambda e, xl_=xl_, c=c: e.dma_start(out=xl_, in_=xall[NPREV + c * 128:NPREV + (c + 1) * 128, :]),
                          writes=[xlt_])
                    S.op("act", lambda e, xx=xx: e.activation(out=junk8, in_=xx, func=AF.Square, accum_out=sq8[:, 0:1]),
                         reads=[xxt], writes=[junk8t, sq8t])
                    rstd_of(r8[:, 0:1], r8t, sq8[:, 0:1], sq8t, D, EPS / 4)
                    S.op("dve", lambda e, xx=xx: e.scalar_tensor_tensor(out=xx, in0=xx, scalar=r8[:, 0:1], in1=gpo,
                                                                        op0=ALU.mult, op1=ALU.mult), reads=[xxt, r8t, gpot], writes=[xxt])
                    S.op("dve", lambda e, xx=xx, xl_=xl_: e.tensor_tensor(out=xx, in0=xx, in1=xl_, op=ALU.add),
                         reads=[xxt, xlt_], writes=[xxt])
                    S.op("act", lambda e, xx=xx: e.activation(out=junk8, in_=xx, func=AF.Square, accum_out=sq8[:, 1:2]),
                         reads=[xxt], writes=[junk8t, sq8t])
                    rstd_of(r8[:, 1:2], r8t, sq8[:, 1:2], sq8t, D, EPS)
                    S.op("dve", lambda e, xx=xx: e.tensor_scalar(out=hnb, in0=xx, scalar1=r8[:, 1:2], scalar2=None, op0=ALU.mult),
                         reads=[xxt, r8t], writes=[hnbt])
                    for half in range(2):
                        pb, pbt = bank()
                        pbb = pb.bitcast(BF16).rearrange("p (a b) -> p a b", a=8)
                        for k in range(8):
                            kt = half * 8 + k
                            tr(pbb[:, k, :], pbt, hnb[:, kt * 128:(kt + 1) * 128], identb, [hnbt, identbt])
                        for k in range(8):
                            kt = half * 8 + k
                            if k % 2:
                                S.op("dve", lambda e, pbb=pbb, k=k, kt=kt, ci=ci: e.tensor_scalar(
                                    out=hnT[:, kt, ci * 128:(ci + 1) * 128], in0=pbb[:, k, :], scalar1=P("gmlp", kt, kt + 1), scalar2=None,
                                    op0=ALU.mult), reads=[pbt, ppt], writes=[hnTt])
                            else:
                                S.op("act", lambda e, pbb=pbb, k=k, kt=kt, ci=ci: e.activation(
                                    out=hnT[:, kt, ci * 128:(ci + 1) * 128], in_=pbb[:, k, :], func=AF.Copy, scale=P("gmlp", kt, kt + 1)),
                                    reads=[pbt, ppt], writes=[hnTt])
                for s in range(16):
                    w1, w1t = p8slab(4 + s, w_ff1[:, s * 512:(s + 1) * 512], tg3)
                    for j in range(4):
                        pb, pbt = bank()
                        for kt in range(16):
                            mm(pb[:, 0:384], pbt, w1[:, kt, j * 128:(j + 1) * 128], hnT[:, kt, :], [w1t, hnTt], start=(kt == 0), stop=(kt == 15))
                        rt_, rtt_ = rtmp[j % 2]
                        S.op("act", lambda e, pb=pb, rt_=rt_: e.activation(out=rt_, in_=pb[:, 0:384], func=AF.Relu), reads=[pbt], writes=[rtt_])
                        S.op("dve", lambda e, rt_=rt_, s=s, j=j: e.tensor_tensor(out=hff[:, s * 4 + j, :], in0=rt_, in1=rt_, op=ALU.mult),
                             reads=[rtt_], writes=[hfft])
                for cb4 in range(4):
                    for rb in range(4):
                        w2, w2t = p8slab(20 + cb4 * 4 + rb, w_ff2[rb * 2048:(rb + 1) * 2048, cb4 * 512:(cb4 + 1) * 512], tg3)
                        for ci in range(3):
                            pb, pbt = banks[5 + ci][0][:, :], banks[5 + ci][1]
                            for kt in range(16):
                                mm(pb, pbt, hff[:, rb * 16 + kt, ci * 128:(ci + 1) * 128], w2[:, kt, :], [hfft, w2t],
                                   start=(rb == 0 and kt == 0), stop=(rb == 3 and kt == 15))
                    for ci in range(3):
                        pb, pbt = banks[5 + ci][0][:, :], banks[5 + ci][1]
                        fo, fot = ffo[ci]
                        S.op("act", lambda e, pb=pb, fo=fo, cb4=cb4: e.copy(out=fo[:, cb4 * 512:(cb4 + 1) * 512], in_=pb),
                             reads=[pbt], writes=[fot])
                for ci in range(3):
                    c = ch0 + ci
                    xx, xxt = x2[ci]
                    fo, fot = ffo[ci]
                    S.op("act", lambda e, fo=fo: e.activation(out=junk8, in_=fo, func=AF.Square, accum_out=sq8[:, 2:3]),
                         reads=[fot], writes=[junk8t, sq8t])
                    rstd_of(r8[:, 2:3], r8t, sq8[:, 2:3], sq8t, D, EPS)
                    S.op("dve", lambda e, fo=fo: e.scalar_tensor_tensor(out=fo, in0=fo, scalar=r8[:, 2:3], in1=gmo,
                                                                        op0=ALU.mult, op1=ALU.mult), reads=[fot, r8t, gmot], writes=[fot])
                    S.op("dve", lambda e, fo=fo, xx=xx: e.tensor_tensor(out=fo, in0=fo, in1=xx, op=ALU.add),
                         reads=[fot, xxt], writes=[fot])
                    S.dma("sp", lambda e, fo=fo, c=c: e.dma_start(out=y_o[c * 128:(c + 1) * 128, :], in_=fo), reads=[fot], tok=fot)

        S.final_wait("sp")
        S.emit()
    return nc


def make_in_maps(inp):
    f32 = lambda a: np.ascontiguousarray(np.asarray(a, dtype=np.float32))
    xp, xs = f32(inp["x_prompt"]), f32(inp["x_sample"])
    shared = {
        "w_in": f32(inp["w_in"][0]), "w_out_a": f32(inp["w_out_a"][0]), "w_out_b": f32(inp["w_out_b"][0]),
        "w_mem_kv": f32(inp["w_mem_kv"][0]), "w_out_x": f32(inp["w_out_x"][0]), "w_o": f32(inp["w_o"][0]),
        "w_ff1": f32(inp["w_ff1"][0]), "w_ff2": f32(inp["w_ff2"][0]),
        "g_pre": f32(inp["norm_mix_pre"][0]).reshape(1, D), "g_post": f32(inp["norm_mix_post"][0]).reshape(1, D),
        "g_mpost": f32(inp["norm_mlp_post"][0]).reshape(1, D),
    }
    maps = []
    for c in range(NCORES):
        b, half = c // 2, c % 2
        sl = slice(16 * c, 16 * (c + 1))
        xall = np.zeros((NTOK, D), np.float32)
        if half == 1:
            xall[0:NPREV] = xp[b, 0:1024]
        xall[NPREV:NPREV + 1024] = xp[b, half * 1024:(half + 1) * 1024]
        xall[NPREV + 1024:] = xs[sl].reshape(128, D)
        m = dict(shared)
        m["xall"] = xall
        m["pp"] = build_pp(float(half), f32(inp["norm_mlp_pre"][0]), f32(inp["ssm_norm_w"][0]), f32(inp["d_skip"][0]),
                           f32(inp["conv_a_w"][0]), f32(inp["conv_b_w"][0]), f32(inp["conv_b_bias"][0]),
                           f32(inp["dt_bias"][0]), f32(inp["a_log"][0]), f32(inp["norm_mem"][0]))
        m["memp"] = f32(inp["mem_prompt"][b])
        m["kc"] = f32(inp["cache_mem_k"][0, sl]).reshape(16, 256, 512)
        m["vc"] = f32(inp["cache_mem_v"][0, sl]).reshape(16, 256, 512)
        m["sca"] = f32(inp["state_conv_a"][0, sl]).reshape(32, D)
        m["scb"] = f32(inp["state_conv_b"][0, sl]).reshape(48, 6144)
        m["sssm"] = f32(inp["state_ssm"][0, sl])
        maps.append(m)
    return maps


_NC_CACHE = {}


def kernel(**inputs):
    if "nc" not in _NC_CACHE:
        _NC_CACHE["nc"] = build_nc()
    nc = _NC_CACHE["nc"]
    maps = make_in_maps(inputs)
    res = run_bass_kernel_spmd(nc, maps, core_ids=list(range(NCORES)))
    r = res.results
    y_p = np.zeros((4, 2048, D), np.float32)
    y_s = np.zeros((128, 8, D), np.float32)
    mk = np.zeros((1, 4, 256, 4, 128), np.float32)
    mv = np.zeros((1, 4, 256, 4, 128), np.float32)
    cap = np.zeros((1, 4, 2, D), np.float32)
    cbp = np.zeros((1, 4, 3, 6144), np.float32)
    ssp = np.zeros((1, 4, 64, 64, 128), np.float32)
    cas = np.zeros((1, 128, 2, D), np.float32)
    cbs = np.zeros((1, 128, 3, 6144), np.float32)
    sss = np.zeros((1, 128, 64, 64, 128), np.float32)
    for c in range(NCORES):
        b, half = c // 2, c % 2
        sl = slice(16 * c, 16 * (c + 1))
        y = r[c]["y"]
        y_p[b, half * 1024:(half + 1) * 1024] = y[0:1024]
        y_s[sl] = y[1024:].reshape(16, 8, D)
        cas[0, sl] = r[c]["cas"]
        cbs[0, sl] = r[c]["cbs"]
        sss[0, sl] = r[c]["sss"]
        if half == 1:
            mk[0, b] = r[c]["memk"].reshape(256, 4, 128)
            mv[0, b] = r[c]["memv"].reshape(256, 4, 128)
            cap[0, b] = r[c]["cap"]
            cbp[0, b] = r[c]["cbp"]
            ssp[0, b] = r[c]["ssp"]
    return (y_p, y_s, mk, mv, cap, cbp, ssp, cas, cbs, sss)
```
